# Optimizing a Trainium2 kernel written in Bass

```python
import jax, jax.numpy as jnp
from jax import lax
import numpy as np

D_MODEL = 2048
BATCH = 16
SEQ = 2048
DEPTH = 4

H_A = 8
HD_A = 128
KVH_A = 1
H_I = 16
D_I = 64
D_I_ROPE = 32
K_MAX = 256
H_B = 16
HD_B = 64
KVH_B = 2
WINDOW = 128
BLOCK = 128
D_FF = 5632
ROPE_THETA = 10000.0
EPS = 1e-6
NEG = -1e30
N_MOD = 9

SPLIT_SIZES = (H_A * HD_A, KVH_A * HD_A, KVH_A * HD_A,
               H_I * D_I, D_I, H_I,
               H_B * HD_B, KVH_B * HD_B, KVH_B * HD_B,
               D_MODEL, D_MODEL)
N_IN = sum(SPLIT_SIZES)

kernel_name = "hybrid_dsa_swa_sink_macaron_adaln"


def _split_points():
    pts, acc = [], 0
    for s in SPLIT_SIZES[:-1]:
        acc += s
        pts.append(acc)
    return pts


def rms_norm(x, g):
    x32 = x.astype(jnp.float32)
    y = x32 * lax.rsqrt(jnp.mean(x32 * x32, axis=-1, keepdims=True) + EPS)
    return (y * g.astype(jnp.float32)).astype(x.dtype)


def rope_cos_sin(positions, dim, dtype):
    inv_freq = ROPE_THETA ** (-jnp.arange(0, dim, 2, dtype=jnp.float32) / dim)
    ang = positions.astype(jnp.float32)[..., None] * inv_freq
    return jnp.cos(ang)[:, :, None, :].astype(dtype), jnp.sin(ang)[:, :, None, :].astype(dtype)


def apply_rope(x, cos, sin):
    x1, x2 = jnp.split(x, 2, axis=-1)
    return jnp.concatenate([x1 * cos - x2 * sin, x2 * cos + x1 * sin], axis=-1)


def modulate(x, shift, scale):
    return x * (1.0 + scale[:, None, :]) + shift[:, None, :]


def swiglu(u, w_gate, w_up, w_down):
    return (jax.nn.silu(u @ w_gate) * (u @ w_up)) @ w_down


def sparse_indexed_attention(q, k, v, q_idx, k_idx, w_idx):
    B, S, H, Dh = q.shape
    n_sel = min(K_MAX, S // 4)
    nb = S // BLOCK
    key_pos = jnp.arange(S)
    gather = jax.vmap(lambda a, idx: a[idx])

    def block(i):
        t0 = i * BLOCK
        qa = lax.dynamic_slice_in_dim(q, t0, BLOCK, axis=1)
        qi = lax.dynamic_slice_in_dim(q_idx, t0, BLOCK, axis=1)
        wi = lax.dynamic_slice_in_dim(w_idx, t0, BLOCK, axis=1)
        tpos = t0 + jnp.arange(BLOCK)
        dots = jnp.einsum('bqhd,bsd->bqhs', qi, k_idx).astype(jnp.float32) * (D_I ** -0.5)
        score = jnp.einsum('bqhs,bqh->bqs', jax.nn.relu(dots), wi.astype(jnp.float32))
        causal = key_pos[None, :] <= tpos[:, None]
        score = jnp.where(causal[None], score, NEG)
        _, sel = lax.top_k(score, n_sel)
        k_sel = gather(k, sel)
        v_sel = gather(v, sel)
        logits = jnp.einsum('bqhd,bqkd->bqhk', qa, k_sel).astype(jnp.float32) * (Dh ** -0.5)
        valid = sel <= tpos[None, :, None]
        logits = jnp.where(valid[:, :, None, :], logits, NEG)
        p = jax.nn.softmax(logits, axis=-1).astype(v.dtype)
        return jnp.einsum('bqhk,bqkd->bqhd', p, v_sel)

    out = lax.map(block, jnp.arange(nb))
    return jnp.moveaxis(out, 0, 1).reshape(B, S, H * Dh)


def sliding_window_attention(q, k, v, sinks):
    B, S, H, Dh = q.shape
    G = k.shape[2]
    R = H // G
    nb = S // BLOCK
    qb = q.reshape(B, nb, BLOCK, G, R, Dh)
    kb = k.reshape(B, nb, BLOCK, G, Dh)
    vb = v.reshape(B, nb, BLOCK, G, Dh)
    pad = jnp.zeros_like(kb[:, :1])
    k_band = jnp.concatenate([jnp.concatenate([pad, kb[:, :-1]], axis=1), kb], axis=2)
    v_band = jnp.concatenate([jnp.concatenate([pad, vb[:, :-1]], axis=1), vb], axis=2)
    logits = jnp.einsum('bnqgrd,bnkgd->bngrqk', qb, k_band).astype(jnp.float32) * (Dh ** -0.5)
    i = jnp.arange(BLOCK)[:, None]
    j = jnp.arange(2 * BLOCK)[None, :]
    rel = i + BLOCK - j
    band = (rel >= 0) & (rel < WINDOW)
    blk = jnp.arange(nb)[:, None, None]
    mask = band[None] & ((blk > 0) | (j >= BLOCK)[None])
    logits = jnp.where(mask[None, :, None, None], logits, NEG)
    sink = sinks.astype(jnp.float32).reshape(G, R)[None, None, :, :, None, None]
    m = jnp.maximum(jnp.max(logits, axis=-1, keepdims=True), sink)
    e = jnp.exp(logits - m)
    p = (e / (jnp.sum(e, axis=-1, keepdims=True) + jnp.exp(sink - m))).astype(v.dtype)
    out = jnp.einsum('bngrqk,bnkgd->bnqgrd', p, v_band)
    return out.reshape(B, S, H * Dh)


def token_mixing(u, positions, w_in, qn_a_g, kn_a_g, qn_b_g, kn_b_g, sinks, w_o_a, w_o_b, w_out):
    B, S, _ = u.shape
    proj = u @ w_in
    qa, ka, va, qi, ki, wi, qb, kb, vb, ga, gb = jnp.split(proj, _split_points(), axis=-1)
    cos_a, sin_a = rope_cos_sin(positions, HD_A, u.dtype)
    cos_b, sin_b = rope_cos_sin(positions, HD_B, u.dtype)
    cos_i, sin_i = rope_cos_sin(positions, D_I_ROPE, u.dtype)
    qa = apply_rope(rms_norm(qa.reshape(B, S, H_A, HD_A), qn_a_g), cos_a, sin_a)
    ka = apply_rope(rms_norm(ka.reshape(B, S, KVH_A, HD_A), kn_a_g), cos_a, sin_a)[:, :, 0]
    qi = qi.reshape(B, S, H_I, D_I)
    qi = jnp.concatenate([apply_rope(qi[..., :D_I_ROPE], cos_i, sin_i), qi[..., D_I_ROPE:]], axis=-1)
    ki = ki.reshape(B, S, 1, D_I)
    ki = jnp.concatenate([apply_rope(ki[..., :D_I_ROPE], cos_i, sin_i), ki[..., D_I_ROPE:]], axis=-1)[:, :, 0]
    wi = wi * (H_I ** -0.5)
    y_a = sparse_indexed_attention(qa, ka, va, qi, ki, wi)
    qb = apply_rope(rms_norm(qb.reshape(B, S, H_B, HD_B), qn_b_g), cos_b, sin_b)
    kb = apply_rope(rms_norm(kb.reshape(B, S, KVH_B, HD_B), kn_b_g), cos_b, sin_b)
    vb = vb.reshape(B, S, KVH_B, HD_B)
    y_b = sliding_window_attention(qb, kb, vb, sinks)
    merged = jax.nn.sigmoid(ga) * (y_a @ w_o_a) + jax.nn.sigmoid(gb) * (y_b @ w_o_b)
    return merged @ w_out


def setup_inputs(seed: int = 0) -> dict:
    key = jax.random.key(seed)
    ks = jax.random.split(key, 24)
    f32 = jnp.float32

    def nrm(k, shape, scale):
        return jax.random.normal(k, shape, f32) * scale

    def gain(k, shape):
        return 1.0 + 0.02 * jax.random.normal(k, shape, f32)

    L, D, F = DEPTH, D_MODEL, D_FF
    return {
        "x": nrm(ks[0], (BATCH, SEQ, D), 1.0),
        "c": nrm(ks[1], (BATCH, D), 1.0),
        "positions": jnp.broadcast_to(jnp.arange(SEQ, dtype=jnp.int32)[None, :], (BATCH, SEQ)),
        "ada_w": nrm(ks[2], (L, D, N_MOD * D), 0.5 * D ** -0.5),
        "ada_b": nrm(ks[3], (L, N_MOD * D), 0.02),
        "norm_ffn1_g": gain(ks[4], (L, D)),
        "ffn1_w_gate": nrm(ks[5], (L, D, F), D ** -0.5),
        "ffn1_w_up": nrm(ks[6], (L, D, F), D ** -0.5),
        "ffn1_w_down": nrm(ks[7], (L, F, D), F ** -0.5),
        "norm_mix_g": gain(ks[8], (L, D)),
        "w_in": nrm(ks[9], (L, D, N_IN), D ** -0.5),
        "qn_a_g": gain(ks[10], (L, HD_A)),
        "kn_a_g": gain(ks[11], (L, HD_A)),
        "qn_b_g": gain(ks[12], (L, HD_B)),
        "kn_b_g": gain(ks[13], (L, HD_B)),
        "sinks": nrm(ks[14], (L, H_B), 1.0),
        "w_o_a": nrm(ks[15], (L, H_A * HD_A, D), (H_A * HD_A) ** -0.5),
        "w_o_b": nrm(ks[16], (L, H_B * HD_B, D), (H_B * HD_B) ** -0.5),
        "w_out": nrm(ks[17], (L, D, D), D ** -0.5),
        "norm_ffn2_g": gain(ks[18], (L, D)),
        "ffn2_w_gate": nrm(ks[19], (L, D, F), D ** -0.5),
        "ffn2_w_up": nrm(ks[20], (L, D, F), D ** -0.5),
        "ffn2_w_down": nrm(ks[21], (L, F, D), F ** -0.5),
    }


def reference(x, c, positions, ada_w, ada_b, norm_ffn1_g, ffn1_w_gate, ffn1_w_up, ffn1_w_down,
              norm_mix_g, w_in, qn_a_g, kn_a_g, qn_b_g, kn_b_g, sinks, w_o_a, w_o_b, w_out,
              norm_ffn2_g, ffn2_w_gate, ffn2_w_up, ffn2_w_down):
    h = x
    c_act = jax.nn.silu(c)
    for l in range(DEPTH):
        mod = c_act @ ada_w[l] + ada_b[l]
        sh1, sc1, g1, sh2, sc2, g2, sh3, sc3, g3 = jnp.split(mod, N_MOD, axis=-1)
        u = modulate(rms_norm(h, norm_ffn1_g[l]), sh1, sc1)
        h = h + 0.5 * g1[:, None, :] * swiglu(u, ffn1_w_gate[l], ffn1_w_up[l], ffn1_w_down[l])
        u = modulate(rms_norm(h, norm_mix_g[l]), sh2, sc2)
        h = h + g2[:, None, :] * token_mixing(u, positions, w_in[l], qn_a_g[l], kn_a_g[l],
                                              qn_b_g[l], kn_b_g[l], sinks[l],
                                              w_o_a[l], w_o_b[l], w_out[l])
        u = modulate(rms_norm(h, norm_ffn2_g[l]), sh3, sc3)
        h = h + 0.5 * g3[:, None, :] * swiglu(u, ffn2_w_gate[l], ffn2_w_up[l], ffn2_w_down[l])
    return h
```

```python
import math
from contextlib import ExitStack

import numpy as np
import concourse.bass as bass
import concourse.mybir as mybir
from concourse.bass_utils import run_bass_kernel_spmd

F32, BF16, I32 = mybir.dt.float32, mybir.dt.bfloat16, mybir.dt.int32
ALU = mybir.AluOpType
AF = mybir.ActivationFunctionType

D = 2048
S = 2048
FF = 5632
L = 4
KC = D // 128
FC = FF // 128
NSEQ = 2
N_IN = 7760
EPS = 1e-6
TT = 1024

O_QA, O_KA, O_VA, O_QI, O_KI, O_WI, O_QB, O_KB, O_VB, O_GA, O_GB = (
    0, 1024, 1152, 1280, 2304, 2368, 2384, 3408, 3536, 3664, 5712)


_SEMS = {}
_CNT = {}


class BlockCtx:
    def __init__(self, nc, name):
        self.nc = nc
        self.name = name
        self.ops = {e: [] for e in ("pe", "act", "dve", "pool", "sp")}
        self.cnt = _CNT.setdefault(id(nc), {})
        self.final = []

    def _t(self, key, inc):
        self.cnt[key] = self.cnt.get(key, 0) + inc
        return (key, self.cnt[key])

    def op(self, eng, fn, deps=(), sig=True):
        t = self._t(eng, 1) if sig else None
        self.ops[eng].append((fn, tuple(d for d in deps if d is not None), t, 1))
        return t

    def dma(self, eng, fn, key, deps=(), final=False):
        t = self._t("q" + key, 16)
        self.ops[eng].append((fn, tuple(d for d in deps if d is not None), t, 16))
        if final:
            self.final.append(t)
        return t

    def run(self):
        nc = self.nc
        if self.final:
            mx = {}
            for k, v in self.final:
                mx[k] = max(mx.get(k, 0), v)
            self.ops["sp"].append((None, tuple(mx.items()), None, 0))
        pool = _SEMS.setdefault(id(nc), {})
        for k in self.cnt:
            if k not in pool:
                pool[k] = nc.alloc_semaphore(f"s_{k}")
        sems = pool
        with ExitStack() as st:
            blk = st.enter_context(nc.Block())
            decos = {"pe": blk.tensor, "act": blk.scalar, "dve": blk.vector,
                     "pool": blk.gpsimd, "sp": blk.sync}
            for eng, lst in self.ops.items():
                if not lst:
                    continue

                def body(e, lst=lst):
                    waited = {}
                    for fn, deps, t, inc in lst:
                        for k, v in deps:
                            if waited.get(k, 0) < v:
                                e.wait_ge(sems[k], v)
                                waited[k] = v
                        if fn is None:
                            continue
                        ins = fn(e)
                        if t is not None:
                            ins.then_inc(sems[t[0]], inc)

                decos[eng](body)


_UID = [0]


def _sb(nc, name, shape, dt):
    _UID[0] += 1
    return nc.sbuf_tensor(f"{name}_{_UID[0]}", shape, dt)


def _ps(nc, name, shape, dt):
    _UID[0] += 1
    return nc.psum_tensor(f"{name}_{_UID[0]}", shape, dt)


class G:
    pass


def bc(ap, shape, axis):
    return ap.unsqueeze(axis).to_broadcast(list(shape))


def blk_init(nc, g):
    c = BlockCtx(nc, "init")
    with _sb(nc, "crow", [32, 128], F32) as crow, \
            _ps(nc, "cps", [128, 32], F32) as cps:
        t1 = c.op("pool", lambda e: e.memset(g.identF[:], 0.0))
        t2 = c.op("pool", lambda e: e.affine_select(
            out=g.identF[:], in_=g.identF[:], compare_op=ALU.not_equal, fill=1.0, base=0,
            pattern=[[-1, 128]], channel_multiplier=1), deps=[t1])
        c.op("pool", lambda e: e.memset(g.onesF[:], 1.0))
        c.op("pool", lambda e: e.memset(g.epsc[:], EPS))
        c.op("dve", lambda e: e.tensor_copy(g.identB[:], g.identF[:]), deps=[t2])
        tl = c.dma("sp", lambda e: e.dma_start(
            out=crow[:], in_=g.c_in.rearrange("b (k p) -> (b k) p", p=128)), "crow")
        tt = c.op("pe", lambda e: e.transpose(cps[:], crow[:], g.identF[0:32, 0:32]), deps=[tl, t2])
        c.op("act", lambda e: e.activation(
            out=g.cactT[:].rearrange("p k b -> p b k"),
            in_=cps[:].rearrange("p (b k) -> p b k", b=NSEQ), func=AF.Silu), deps=[tt])
        c.run()


def blk_pre(nc, g, b):
    c = BlockCtx(nc, f"pre{b}")
    with ExitStack() as st:
        xin = [st.enter_context(_sb(nc, f"xin{i}", [128, 4, D], F32)) for i in range(2)]
        xo = [st.enter_context(_sb(nc, f"xo{i}", [128, KC, 512], F32)) for i in range(2)]
        tp = [st.enter_context(_ps(nc, f"tp{i}", [128, 512], F32)) for i in range(4)]
        xin_rd = [None, None]
        xo_st = [None, None]
        tp_free = [None] * 4
        n = 0
        for i in range(S // 512):
            bf = i % 2
            tok0 = i * 512
            t_ld = c.dma("sp", lambda e, bf=bf, tok0=tok0: e.dma_start(
                out=xin[bf][:], in_=g.x_in[b, tok0:tok0 + 512, :].rearrange("(j p) d -> p j d", p=128)),
                f"xin{bf}", deps=[xin_rd[bf]])
            cps = []
            for k in range(KC):
                pb = n % 4
                n += 1
                for j in range(4):
                    t_tr = c.op("pe", lambda e, pb=pb, j=j, k=k, bf=bf: e.transpose(
                        tp[pb][:, j * 128:(j + 1) * 128], xin[bf][:, j, k * 128:(k + 1) * 128], g.identF[:]),
                        deps=[t_ld, tp_free[pb]], sig=(j == 3))
                eng = "act" if k % 2 == 0 else "dve"
                if eng == "act":
                    t_cp = c.op("act", lambda e, pb=pb, k=k, bf=bf: e.copy(xo[bf][:, k, :], tp[pb][:]),
                                deps=[t_tr, xo_st[bf]])
                else:
                    t_cp = c.op("dve", lambda e, pb=pb, k=k, bf=bf: e.tensor_copy(xo[bf][:, k, :], tp[pb][:]),
                                deps=[t_tr, xo_st[bf]])
                tp_free[pb] = t_cp
                cps.append(t_cp)
            xin_rd[bf] = t_tr
            xo_st[bf] = c.dma("sp", lambda e, bf=bf, tok0=tok0: e.dma_start(
                out=g.hT[b, :, tok0:tok0 + 512].rearrange("(k p) t -> p k t", p=128), in_=xo[bf][:]),
                f"xo{bf}", deps=cps[-2:], final=True)
        c.run()


def blk_post(nc, g, b):
    c = BlockCtx(nc, f"post{b}")
    with ExitStack() as st:
        hin = [st.enter_context(_sb(nc, f"hin{i}", [128, KC, 512], F32)) for i in range(2)]
        yo = [st.enter_context(_sb(nc, f"yo{i}", [128, 4, D], F32)) for i in range(2)]
        tp = [st.enter_context(_ps(nc, f"tq{i}", [128, 512], F32)) for i in range(4)]
        hin_rd = [None, None]
        yo_st = [None, None]
        tp_free = [None] * 4
        n = 0
        for i in range(S // 512):
            bf = i % 2
            tok0 = i * 512
            t_ld = c.dma("sp", lambda e, bf=bf, tok0=tok0: e.dma_start(
                out=hin[bf][:], in_=g.hT[b, :, tok0:tok0 + 512].rearrange("(k p) t -> p k t", p=128)),
                f"hin{bf}", deps=[hin_rd[bf]])
            cps = []
            for j in range(4):
                for kg in range(4):
                    pb = n % 4
                    n += 1
                    for kk in range(4):
                        k = kg * 4 + kk
                        t_tr = c.op("pe", lambda e, pb=pb, j=j, k=k, kk=kk, bf=bf: e.transpose(
                            tp[pb][:, kk * 128:(kk + 1) * 128], hin[bf][:, k, j * 128:(j + 1) * 128],
                            g.identF[:]), deps=[t_ld, tp_free[pb]], sig=(kk == 3))
                    if n % 2 == 0:
                        t_cp = c.op("act", lambda e, pb=pb, j=j, kg=kg, bf=bf: e.copy(
                            yo[bf][:, j, kg * 512:(kg + 1) * 512], tp[pb][:]), deps=[t_tr, yo_st[bf]])
                    else:
                        t_cp = c.op("dve", lambda e, pb=pb, j=j, kg=kg, bf=bf: e.tensor_copy(
                            yo[bf][:, j, kg * 512:(kg + 1) * 512], tp[pb][:]), deps=[t_tr, yo_st[bf]])
                    tp_free[pb] = t_cp
                    cps.append(t_cp)
            hin_rd[bf] = t_tr
            yo_st[bf] = c.dma("sp", lambda e, bf=bf, tok0=tok0: e.dma_start(
                out=g.out[b, tok0:tok0 + 512, :].rearrange("(j p) d -> p j d", p=128), in_=yo[bf][:]),
                f"yo{bf}", deps=cps[-2:], final=True)
        c.run()


def blk_mod(nc, g, l):
    c = BlockCtx(nc, f"mod{l}")
    NTILE = 9 * D // 512
    with ExitStack() as st:
        rowsA = st.enter_context(_sb(nc, "rowsA", [128, 128], F32))
        rowsB = st.enter_context(_sb(nc, "rowsB", [64, 128], F32))
        abT = st.enter_context(_sb(nc, "abT", [128, 144], F32))
        wt = [st.enter_context(_sb(nc, f"adaw{i}", [128, KC, 512], BF16)) for i in range(3)]
        rps = st.enter_context(_ps(nc, "rps", [128, 192], F32))
        mps = st.enter_context(_ps(nc, "mps", [128, 288], F32))
        ab = g.ada_b[l].rearrange("(r p) -> r p", p=128)
        lds = [
            c.dma("sp", lambda e: e.dma_start(out=rowsA[:], in_=ab[0:128, :]), "rows"),
            c.dma("sp", lambda e: e.dma_start(out=rowsB[0:16, :], in_=ab[128:144, :]), "rows"),
            c.dma("sp", lambda e: e.dma_start(
                out=rowsB[16:32, :], in_=g.norm_g[0][l].rearrange("(r p) -> r p", p=128)), "rows"),
            c.dma("sp", lambda e: e.dma_start(
                out=rowsB[32:48, :], in_=g.norm_g[1][l].rearrange("(r p) -> r p", p=128)), "rows"),
            c.dma("sp", lambda e: e.dma_start(
                out=rowsB[48:64, :], in_=g.norm_g[2][l].rearrange("(r p) -> r p", p=128)), "rows"),
        ]
        tA = c.op("pe", lambda e: e.transpose(rps[:, 0:128], rowsA[:], g.identF[:]), deps=[lds[-1]], sig=False)
        tB = c.op("pe", lambda e: e.transpose(rps[:, 128:192], rowsB[:], g.identF[0:64, 0:64]), deps=[lds[-1]])
        t_ab = c.op("dve", lambda e: e.tensor_copy(abT[:], rps[:, 0:144]), deps=[tB])
        t_ng = c.op("dve", lambda e: e.tensor_copy(g.normg[:], rps[:, 144:192]), deps=[tB])
        w_rd = [None] * 3
        t_mm = None
        for ti in range(NTILE):
            bf = ti % 3
            t_w = c.dma("pool", lambda e, bf=bf, ti=ti: e.dma_start(
                out=wt[bf][:], in_=g.ada_w[l, :, ti * 512:(ti + 1) * 512].rearrange("(k p) n -> p k n", p=128)),
                f"adaw{bf}", deps=[w_rd[bf]])
            for n in range(4):
                nch = ti * 4 + n
                for k in range(KC):
                    t_mm = c.op("pe", lambda e, bf=bf, n=n, k=k, nch=nch: e.matmul(
                        mps[:, nch * 2:nch * 2 + 2], wt[bf][:, k, n * 128:(n + 1) * 128], g.cactT[:, k, :],
                        start=(k == 0), stop=(k == KC - 1)), deps=[t_w], sig=(n == 3 and k == KC - 1))
            w_rd[bf] = t_mm
        t_mod = c.op("dve", lambda e: e.tensor_tensor(
            out=g.mod[:], in0=mps[:].rearrange("p (m b) -> p m b", b=NSEQ),
            in1=bc(abT[:], [128, 144, NSEQ], 2), op=ALU.add), deps=[t_mm, t_ab])
        for j in range(3):
            c.op("dve", lambda e, j=j: e.scalar_tensor_tensor(
                out=g.gsc[:, j], in0=g.mod[:, (3 * j + 1) * KC:(3 * j + 2) * KC, :], scalar=1.0,
                in1=bc(g.normg[:, j * KC:(j + 1) * KC], [128, KC, NSEQ], 2), op0=ALU.add, op1=ALU.mult),
                deps=[t_mod, t_ng])
            c.op("dve", lambda e, j=j: e.tensor_scalar_mul(
                g.gt[:, j], g.mod[:, (3 * j + 2) * KC:(3 * j + 3) * KC, :], 1.0 if j == 1 else 0.5),
                deps=[t_mod])
        c.run()


def blk_norm(nc, g, name, b, j, tok0, ntok, u):
    c = BlockCtx(nc, name)
    P = 256
    with ExitStack() as st:
        hs = [st.enter_context(_sb(nc, f"hs{i}", [128, KC, P], F32)) for i in range(2)]
        sq = [st.enter_context(_sb(nc, f"sq{i}", [128, KC, P], F32)) for i in range(2)]
        rstd = [st.enter_context(_sb(nc, f"rstd{i}", [128, P], F32)) for i in range(2)]
        ssp = [st.enter_context(_ps(nc, f"ssp{i}", [128, P], F32)) for i in range(2)]
        hs_free = [None, None]
        sq_free = [None, None]
        rstd_free = [None, None]
        ssp_free = [None, None]
        for i in range(ntok // P):
            bf = i % 2
            t0 = tok0 + i * P
            t_ld = c.dma("sp", lambda e, bf=bf, t0=t0: e.dma_start(
                out=hs[bf][:], in_=g.hT[b, :, t0:t0 + P].rearrange("(k p) t -> p k t", p=128)),
                f"hs{bf}", deps=[hs_free[bf]])
            t_sq = c.op("act", lambda e, bf=bf: e.activation(out=sq[bf][:], in_=hs[bf][:], func=AF.Square),
                        deps=[t_ld, sq_free[bf]])
            for k in range(KC):
                t_ss = c.op("pe", lambda e, bf=bf, k=k: e.matmul(
                    ssp[bf][:], g.onesF[:], sq[bf][:, k, :], start=(k == 0), stop=(k == KC - 1)),
                    deps=[t_sq, ssp_free[bf]], sig=(k == KC - 1))
            sq_free[bf] = t_ss
            t_r1 = c.op("act", lambda e, bf=bf: e.activation(
                out=rstd[bf][:], in_=ssp[bf][:], func=AF.Sqrt, scale=1.0 / D, bias=g.epsc[:]),
                deps=[t_ss, rstd_free[bf]])
            ssp_free[bf] = t_r1
            t_r2 = c.op("dve", lambda e, bf=bf: e.reciprocal(rstd[bf][:], rstd[bf][:]), deps=[t_r1])
            t_a = c.op("dve", lambda e, bf=bf: e.tensor_tensor(
                out=hs[bf][:], in0=hs[bf][:], in1=bc(rstd[bf][:], [128, KC, P], 1), op=ALU.mult),
                deps=[t_r2, t_sq])
            t_b = c.op("pool", lambda e, bf=bf: e.tensor_tensor(
                out=hs[bf][:], in0=hs[bf][:], in1=bc(g.gsc[:, j, :, b], [128, KC, P], 2), op=ALU.mult),
                deps=[t_a])
            rstd_free[bf] = t_a
            t_c = c.op("dve", lambda e, bf=bf, i=i: e.tensor_tensor(
                out=u[:, :, i * P:(i + 1) * P], in0=hs[bf][:],
                in1=bc(g.mod[:, 3 * j * KC:(3 * j + 1) * KC, b], [128, KC, P], 2), op=ALU.add),
                deps=[t_b])
            hs_free[bf] = t_c
        c.run()


def blk_gu(nc, g, name, wg_ap, wu_ap, u, hidden, ntok):
    c = BlockCtx(nc, name)
    NH = ntok // 512
    with ExitStack() as st:
        wg = [st.enter_context(_sb(nc, f"wg{i}", [128, KC, 256], BF16)) for i in range(NWB)]
        wu = [st.enter_context(_sb(nc, f"wu{i}", [128, KC, 256], BF16)) for i in range(NWB)]
        sg = [st.enter_context(_sb(nc, f"sg{i}", [128, ntok], F32)) for i in range(2)]
        gps = [st.enter_context(_ps(nc, f"gps{i}", [128, ntok], F32)) for i in range(2)]
        ups = [st.enter_context(_ps(nc, f"ups{i}", [128, ntok], F32)) for i in range(2)]
        w_rd = [None] * NWB
        t_silu = [None, None]
        t_mul = [None, None]
        t_w = None
        for fc in range(FC):
            stg, col, pp = fc // 2, (fc % 2) * 128, fc % 2
            bf = stg % NWB
            if fc % 2 == 0:
                c.dma("pool", lambda e, bf=bf, stg=stg: e.dma_start(
                    out=wg[bf][:], in_=wg_ap[:, stg * 256:(stg + 1) * 256].rearrange("(k p) f -> p k f", p=128)),
                    f"w{bf}", deps=[w_rd[bf]])
                t_w = c.dma("pool", lambda e, bf=bf, stg=stg: e.dma_start(
                    out=wu[bf][:], in_=wu_ap[:, stg * 256:(stg + 1) * 256].rearrange("(k p) f -> p k f", p=128)),
                    f"w{bf}", deps=[w_rd[bf]])
            for h in range(NH):
                for k in range(KC):
                    t_g = c.op("pe", lambda e, pp=pp, h=h, k=k, bf=bf, col=col: e.matmul(
                        gps[pp][:, h * 512:(h + 1) * 512], wg[bf][:, k, col:col + 128],
                        u[:, k, h * 512:(h + 1) * 512], start=(k == 0), stop=(k == KC - 1)),
                        deps=[t_w, t_silu[pp]], sig=(h == NH - 1 and k == KC - 1))
            for h in range(NH):
                for k in range(KC):
                    t_u = c.op("pe", lambda e, pp=pp, h=h, k=k, bf=bf, col=col: e.matmul(
                        ups[pp][:, h * 512:(h + 1) * 512], wu[bf][:, k, col:col + 128],
                        u[:, k, h * 512:(h + 1) * 512], start=(k == 0), stop=(k == KC - 1)),
                        deps=[t_w, t_mul[pp]], sig=(h == NH - 1 and k == KC - 1))
            if fc % 2 == 1:
                w_rd[bf] = t_u
            t_silu[pp] = c.op("act", lambda e, pp=pp: e.activation(out=sg[pp][:], in_=gps[pp][:], func=AF.Silu),
                              deps=[t_g, t_mul[pp]])
            t_mul[pp] = c.op("dve", lambda e, pp=pp, fc=fc: e.tensor_tensor(
                out=hidden[:, fc, 0:ntok], in0=sg[pp][:], in1=ups[pp][:], op=ALU.mult),
                deps=[t_silu[pp], t_u])
        c.run()


def blk_down(nc, g, name, wd_ap, hidden, b, j, tok0, ntok, nfc=FC):
    c = BlockCtx(nc, name)
    NH = ntok // 512
    with ExitStack() as st:
        wd = [st.enter_context(_sb(nc, f"wd{i}", [128, nfc, 256], BF16)) for i in range(2)]
        qsz = 11 if nfc % 11 == 0 else 8
        hr = [st.enter_context(_sb(nc, f"hr{i}", [128, ntok], F32)) for i in range(3)]
        dps = [st.enter_context(_ps(nc, f"dps{i}", [128, ntok], F32)) for i in range(2)]
        w_rd = [None, None]
        hr_st = [None] * 3
        dps_free = [None, None]
        t_w = None
        for n in range(KC):
            stg, col, pp = n // 2, (n % 2) * 128, n % 2
            bf = stg % 2
            rb = n % 3
            if n % 2 == 0:
                for q in range(nfc // qsz):
                    t_w = c.dma("pool", lambda e, bf=bf, stg=stg, q=q: e.dma_start(
                        out=wd[bf][:, q * qsz:(q + 1) * qsz, :],
                        in_=wd_ap[q * qsz * 128:(q + 1) * qsz * 128, stg * 256:(stg + 1) * 256].rearrange(
                            "(f p) n -> p f n", p=128)),
                        f"wd{bf}", deps=[w_rd[bf]])
            t_h = c.dma("sp", lambda e, rb=rb, n=n: e.dma_start(
                out=hr[rb][:], in_=g.hT[b, n * 128:(n + 1) * 128, tok0:tok0 + ntok]),
                f"hr{rb}", deps=[hr_st[rb]])
            for h in range(NH):
                for f in range(nfc):
                    t_d = c.op("pe", lambda e, pp=pp, h=h, f=f, bf=bf, col=col: e.matmul(
                        dps[pp][:, h * 512:(h + 1) * 512], wd[bf][:, f, col:col + 128],
                        hidden[:, f, h * 512:(h + 1) * 512], start=(f == 0), stop=(f == nfc - 1)),
                        deps=[t_w, dps_free[pp]], sig=(h == NH - 1 and f == nfc - 1))
            if n % 2 == 1:
                w_rd[bf] = t_d
            t_r = c.op("dve", lambda e, pp=pp, rb=rb, n=n: e.scalar_tensor_tensor(
                out=hr[rb][:], in0=dps[pp][:], scalar=g.gt[:, j, n, b:b + 1], in1=hr[rb][:],
                op0=ALU.mult, op1=ALU.add), deps=[t_d, t_h])
            dps_free[pp] = t_r
            hr_st[rb] = c.dma("sp", lambda e, rb=rb, n=n: e.dma_start(
                out=g.hT[b, n * 128:(n + 1) * 128, tok0:tok0 + ntok], in_=hr[rb][:]),
                f"hr{rb}", deps=[t_r], final=True)
        c.run()


NEG = -1e30
TWO_PI = 2.0 * math.pi
C1 = 6.28125
C2 = TWO_PI - C1


def host_consts():
    d = np.arange(128)
    th = 10000.0
    cols = np.zeros((128, 8), np.float32)
    cols[:, 0] = (np.float32(th) ** (-(2.0 * (d % 64)).astype(np.float32) / np.float32(128))).astype(np.float32)
    cols[:, 1] = (np.float32(th) ** (-(2.0 * (d % 32)).astype(np.float32) / np.float32(64))).astype(np.float32)
    fi = (np.float32(th) ** (-(2.0 * (d % 16)).astype(np.float32) / np.float32(32))).astype(np.float32)
    cols[:, 2] = np.where((d % 64) < 32, fi, 0.0)
    cols[:, 3] = np.where(d < 64, -1.0, 1.0)
    cols[:, 4] = np.where((d % 64) < 32, -1.0, 1.0)
    cols[:, 5] = np.where((d % 64) < 16, -1.0, 1.0)
    return cols


def blk_init2(nc, g):
    c = BlockCtx(nc, "init2")

    def sel(tile_ap, base, cmp=ALU.not_equal, fill=1.0):
        return lambda e: e.affine_select(out=tile_ap, in_=tile_ap, compare_op=cmp, fill=fill, base=base,
                                         pattern=[[-1, tile_ap.shape[1]]], channel_multiplier=1)
    t = c.op("pool", lambda e: e.memset(g.permA[:], 0.0))
    t = c.op("pool", sel(g.permA[:], -64), deps=[t])
    t = c.op("pool", sel(g.permA[:], 64), deps=[t])
    t = c.op("pool", lambda e: e.memset(g.permB[:], 0.0))
    for jb in range(4):
        base = -(32 * jb) - 32 if jb % 2 == 0 else -(32 * jb) + 32
        t = c.op("pool", sel(g.permB[:, 32 * jb:32 * jb + 32], base), deps=[t])
    t = c.op("pool", lambda e: e.memset(g.permI[:], 0.0))
    for jb in range(8):
        if jb % 4 == 0:
            t = c.op("pool", sel(g.permI[:, 16 * jb:16 * jb + 16], -(16 * jb) - 16), deps=[t])
        elif jb % 4 == 1:
            t = c.op("pool", sel(g.permI[:, 16 * jb:16 * jb + 16], -(16 * jb) + 16), deps=[t])
    t = c.op("pool", lambda e: e.memset(g.onesBD[:], 0.0))
    t = c.op("pool", lambda e: e.memset(g.onesBD[0:64, 0:64], 1.0), deps=[t])
    t = c.op("pool", lambda e: e.memset(g.onesBD[64:128, 64:128], 1.0), deps=[t])
    c.op("pool", lambda e: e.memset(g.onesB[:], 1.0))
    t = c.op("pool", lambda e: e.memset(g.cbias[:], 0.0))
    t = c.op("pool", sel(g.cbias[:], 0, ALU.is_ge, NEG), deps=[t])
    t = c.op("pool", lambda e: e.memset(g.mcurF[:], 1.0))
    t = c.op("pool", lambda e: e.affine_select(out=g.mcurF[:], in_=g.mcurF[:], compare_op=ALU.is_ge, fill=0.0,
                                               base=0, pattern=[[1, 128]], channel_multiplier=-1), deps=[t])
    c.op("dve", lambda e: e.tensor_copy(g.mcur[:], g.mcurF[:]), deps=[t])
    t = c.op("pool", lambda e: e.memset(g.mprevF[:], 1.0))
    t = c.op("pool", lambda e: e.affine_select(out=g.mprevF[:], in_=g.mprevF[:], compare_op=ALU.is_gt, fill=0.0,
                                               base=0, pattern=[[-1, 128]], channel_multiplier=1), deps=[t])
    c.op("dve", lambda e: e.tensor_copy(g.mprev[:], g.mprevF[:]), deps=[t])
    c.dma("sp", lambda e: e.dma_start(out=g.cols[:], in_=g.consts_in), "cols", final=True)
    c.run()


def blk_rope(nc, g, b):
    c = BlockCtx(nc, f"rope{b}")
    with ExitStack() as st:
        posi = st.enter_context(_sb(nc, "posi", [128, S], I32))
        posf = st.enter_context(_sb(nc, "posf", [128, S], F32))
        ang = st.enter_context(_sb(nc, "ang", [128, S], F32))
        ki = st.enter_context(_sb(nc, "ki", [128, S], I32))
        kf = st.enter_context(_sb(nc, "kf", [128, S], F32))
        r = st.enter_context(_sb(nc, "r", [128, S], F32))
        m = st.enter_context(_sb(nc, "m", [128, S], F32))
        res = [st.enter_context(_sb(nc, f"res{i}", [128, S], F32)) for i in range(2)]
        tl = c.dma("sp", lambda e: e.dma_start(out=posi[:], in_=g.pos_in[b:b + 1, :].to_broadcast([128, S])), "pos")
        t = c.op("dve", lambda e: e.tensor_copy(posf[:], posi[:]), deps=[tl])
        res_st = [None, None]
        n = 0
        for ty in range(3):
            for which in range(2):
                rb = n % 2
                shift = math.pi / 2 if which == 0 else 0.0
                t = c.op("dve", lambda e, ty=ty, shift=shift: e.tensor_scalar(
                    out=ang[:], in0=posf[:], scalar1=g.cols[:, ty:ty + 1], scalar2=shift,
                    op0=ALU.mult, op1=ALU.add), deps=[t])
                t = c.op("dve", lambda e: e.tensor_scalar(
                    out=ki[:], in0=ang[:], scalar1=1.0 / TWO_PI, scalar2=None, op0=ALU.mult), deps=[t])
                t = c.op("dve", lambda e: e.tensor_copy(kf[:], ki[:]), deps=[t])
                t = c.op("dve", lambda e: e.scalar_tensor_tensor(
                    out=r[:], in0=kf[:], scalar=-C1, in1=ang[:], op0=ALU.mult, op1=ALU.add), deps=[t])
                t = c.op("dve", lambda e: e.scalar_tensor_tensor(
                    out=r[:], in0=kf[:], scalar=-C2, in1=r[:], op0=ALU.mult, op1=ALU.add), deps=[t])
                t = c.op("dve", lambda e: e.tensor_scalar(
                    out=m[:], in0=r[:], scalar1=math.pi, scalar2=-TWO_PI, op0=ALU.is_gt, op1=ALU.mult), deps=[t])
                t = c.op("dve", lambda e: e.tensor_tensor(out=r[:], in0=r[:], in1=m[:], op=ALU.add), deps=[t])
                t = c.op("dve", lambda e: e.tensor_scalar(
                    out=m[:], in0=r[:], scalar1=-math.pi, scalar2=TWO_PI, op0=ALU.is_lt, op1=ALU.mult), deps=[t])
                t = c.op("dve", lambda e: e.tensor_tensor(out=r[:], in0=r[:], in1=m[:], op=ALU.add), deps=[t])
                t = c.op("dve", lambda e: e.tensor_scalar(
                    out=r[:], in0=r[:], scalar1=math.pi, scalar2=-math.pi, op0=ALU.min, op1=ALU.max), deps=[t])
                ta = c.op("act", lambda e, rb=rb: e.activation(out=res[rb][:], in_=r[:], func=AF.Sin),
                          deps=[t, res_st[rb]])
                t = ta
                if which == 1:
                    ta = c.op("dve", lambda e, rb=rb, ty=ty: e.tensor_scalar(
                        out=res[rb][:], in0=res[rb][:], scalar1=g.cols[:, 3 + ty:4 + ty], scalar2=None,
                        op0=ALU.mult), deps=[ta])
                    t = ta
                res_st[rb] = c.dma("sp", lambda e, rb=rb, ty=ty, which=which: e.dma_start(
                    out=g.rope[b, 2 * ty + which], in_=res[rb][:]), f"res{rb}", deps=[ta], final=True)
                n += 1
        c.run()


def blk_proj(nc, g, name, l, b, tok0, u):
    c = BlockCtx(nc, name)
    win = g.w["w_in"][l]
    NP = TT // 512
    groups = [("qa", O_QA, 8, 0, 128, g.qaT[b]), ("ka", O_KA, 1, 0, 128, g.kaT[b]),
              ("qi", O_QI, 8, 2, 0, g.qiT[b]), ("ki", O_KI, 1, 2, 0, g.kiT[b]),
              ("qb", O_QB, 8, 1, 64, g.qbT[b]), ("kb", O_KB, 1, 1, 64, g.kbT[b])]
    with ExitStack() as st:
        wt = [st.enter_context(_sb(nc, f"pw{i}", [128, KC, 512], BF16)) for i in range(2)]
        wv = st.enter_context(_sb(nc, "pwv", [128, KC, 272], BF16))
        gcol = st.enter_context(_sb(nc, "gcol", [128, 4], F32))
        grow = st.enter_context(_sb(nc, "grow", [128, 4], F32))
        tabs = [st.enter_context(_sb(nc, f"tab{i}", [128, 2, TT], F32)) for i in range(3)]
        sq = [st.enter_context(_sb(nc, f"psq{i}", [128, 512], F32)) for i in range(2)]
        xn = [st.enter_context(_sb(nc, f"pxn{i}", [128, 512], F32)) for i in range(2)]
        rs = [st.enter_context(_sb(nc, f"prs{i}", [128, 512], F32)) for i in range(2)]
        t1 = [st.enter_context(_sb(nc, f"pt1{i}", [128, 512], F32)) for i in range(2)]
        ob = [st.enter_context(_sb(nc, f"pob{i}", [128, 512], BF16)) for i in range(3)]
        vo = [st.enter_context(_sb(nc, f"pvo{i}", [128, 272], BF16)) for i in range(2)]
        wo = [st.enter_context(_sb(nc, f"pwo{i}", [128, 16], F32)) for i in range(2)]
        pp = [st.enter_context(_ps(nc, f"ppp{i}", [128, 512], F32)) for i in range(2)]
        ssp = [st.enter_context(_ps(nc, f"pss{i}", [128, 512], F32)) for i in range(2)]
        swp = [st.enter_context(_ps(nc, f"psw{i}", [128, 512], F32)) for i in range(2)]
        vps = [st.enter_context(_ps(nc, f"pvp{i}", [128, 512], F32)) for i in range(2)]
        growT = st.enter_context(_sb(nc, "growT", [4, 128], F32))
        gl = [
            c.dma("sp", lambda e: e.dma_start(out=growT[0:1, :], in_=g.w["qn_a_g"][l:l + 1, :]), "gr"),
            c.dma("sp", lambda e: e.dma_start(out=growT[1:2, :], in_=g.w["kn_a_g"][l:l + 1, :]), "gr"),
            c.dma("sp", lambda e: e.dma_start(out=growT[2:3, 0:64], in_=g.w["qn_b_g"][l:l + 1, :]), "gr"),
            c.dma("sp", lambda e: e.dma_start(out=growT[2:3, 64:128], in_=g.w["qn_b_g"][l:l + 1, :]), "gr"),
            c.dma("sp", lambda e: e.dma_start(out=growT[3:4, 0:64], in_=g.w["kn_b_g"][l:l + 1, :]), "gr"),
            c.dma("sp", lambda e: e.dma_start(out=growT[3:4, 64:128], in_=g.w["kn_b_g"][l:l + 1, :]), "gr"),
        ]
        t_gt = c.op("pe", lambda e: e.transpose(pp[0][:, 0:4], growT[:], g.identF[0:4, 0:4]), deps=[gl[-1]])
        t_gc = c.op("dve", lambda e: e.tensor_copy(grow[:], pp[0][:, 0:4]), deps=[t_gt])
        tg = c.op("dve", lambda e: e.tensor_scalar_mul(gcol[:, 0:1], grow[:, 0:1], 128.0 ** -0.5), deps=[t_gc])
        tg = c.op("dve", lambda e: e.tensor_copy(gcol[:, 1:2], grow[:, 1:2]), deps=[t_gc])
        tg = c.op("dve", lambda e: e.tensor_scalar_mul(gcol[:, 2:3], grow[:, 2:3], 64.0 ** -0.5), deps=[t_gc])
        tg = c.op("dve", lambda e: e.tensor_copy(gcol[:, 3:4], grow[:, 3:4]), deps=[t_gc])
        ttab = None
        for ty in range(3):
            for which in range(2):
                ttab = c.dma("sp", lambda e, ty=ty, which=which: e.dma_start(
                    out=tabs[ty][:, which, :], in_=g.rope[b, 2 * ty + which, :, tok0:tok0 + TT]), "tab")
        w_rd = [None, None]
        cnt = {"w": 0, "u": 0}
        free = {"pp": [t_gc, None], "ssp": [None, None], "swp": [None, None], "sq": [None, None],
                "xn": [None, None], "rs": [None, None], "t1": [None, None], "ob": [None] * 3}
        for (gname, col0, nch, ty, hd, dest) in groups:
            if gname not in PROJ_DBG:
                continue
            gi = {"qa": 0, "ka": 1, "qb": 2, "kb": 3}.get(gname, None)
            for c0 in range(0, nch, 4):
                ncc = min(4, nch - c0)
                bf = cnt["w"] % 2
                cnt["w"] += 1
                width = ncc * 128 if gname != "ki" else 64
                t_w = c.dma("pool", lambda e, bf=bf, col0=col0, c0=c0, width=width: e.dma_start(
                    out=wt[bf][:, :, 0:width],
                    in_=win[:, col0 + c0 * 128:col0 + c0 * 128 + width].rearrange("(k p) n -> p k n", p=128)),
                    f"pw{bf}", deps=[w_rd[bf]])
                for cc in range(ncc):
                    ch = c0 + cc
                    M = 128 if gname != "ki" else 64
                    for pi in range(NP):
                        i = cnt["u"] % 2
                        i3 = cnt["u"] % 3
                        cnt["u"] += 1
                        tsl = slice(pi * 512, (pi + 1) * 512)
                        for k in range(KC):
                            t_p = c.op("pe", lambda e, i=i, bf=bf, cc=cc, k=k, tsl=tsl, M=M: e.matmul(
                                pp[i][0:M, :], wt[bf][:, k, cc * 128:cc * 128 + M], u[:, k, tsl],
                                start=(k == 0), stop=(k == KC - 1)),
                                deps=[t_w, free["pp"][i]], sig=(k == KC - 1))
                        w_last = t_p
                        if hd:
                            t_sq = c.op("act", lambda e, i=i: e.activation(out=sq[i][:], in_=pp[i][:], func=AF.Square),
                                        deps=[t_p, free["sq"][i]])
                            ones = g.onesF if hd == 128 else g.onesBD
                            t_ss = c.op("pe", lambda e, i=i, ones=ones: e.matmul(
                                ssp[i][:], ones[:], sq[i][:], start=True, stop=True), deps=[t_sq, free["ssp"][i]])
                            free["sq"][i] = t_ss
                            t_r = c.op("act", lambda e, i=i, hd=hd: e.activation(
                                out=rs[i][:], in_=ssp[i][:], func=AF.Sqrt, scale=1.0 / hd, bias=g.epsc[:]),
                                deps=[t_ss, free["rs"][i]])
                            free["ssp"][i] = t_r
                            t_r = c.op("dve", lambda e, i=i: e.reciprocal(rs[i][:], rs[i][:]), deps=[t_r])
                            t_x = c.op("dve", lambda e, i=i, gi=gi: e.scalar_tensor_tensor(
                                out=xn[i][:], in0=pp[i][:], scalar=gcol[:, gi:gi + 1], in1=rs[i][:],
                                op0=ALU.mult, op1=ALU.mult), deps=[t_r, tg, free["xn"][i]])
                            free["rs"][i] = t_x
                        else:
                            sc_ = 0.125 if gname == "qi" else 1.0
                            t_x = c.op("act", lambda e, i=i, sc_=sc_, M=M: e.activation(
                                out=xn[i][0:M, :], in_=pp[i][0:M, :], func=AF.Copy, scale=sc_),
                                deps=[t_p, free["xn"][i]])
                        free["pp"][i] = t_x
                        perm = [g.permA, g.permB, g.permI][ty]
                        t_s = c.op("pe", lambda e, i=i, perm=perm, M=M: e.matmul(
                            swp[i][0:M, :], perm[0:M, 0:M], xn[i][0:M, :], start=True, stop=True),
                            deps=[t_x, free["swp"][i]])
                        t_a = c.op("pool", lambda e, i=i, ty=ty, tsl=tsl, M=M: e.tensor_tensor(
                            out=t1[i][0:M, :], in0=xn[i][0:M, :], in1=tabs[ty][0:M, 0, tsl], op=ALU.mult),
                            deps=[t_x, ttab, free["t1"][i]])
                        t_b = c.op("dve", lambda e, i=i, ty=ty, tsl=tsl, M=M: e.tensor_tensor(
                            out=xn[i][0:M, :], in0=swp[i][0:M, :], in1=tabs[ty][0:M, 1, tsl], op=ALU.mult),
                            deps=[t_s, ttab, t_a])
                        free["swp"][i] = t_b
                        t_o = c.op("dve", lambda e, i=i, i3=i3, M=M: e.tensor_tensor(
                            out=ob[i3][0:M, :], in0=t1[i][0:M, :], in1=xn[i][0:M, :], op=ALU.add),
                            deps=[t_b, free["ob"][i3]])
                        free["t1"][i] = t_o
                        free["xn"][i] = t_o
                        if gname == "ki":
                            dst = dest[0:64, tok0 + pi * 512:tok0 + (pi + 1) * 512]
                        elif nch == 1:
                            dst = dest[:, tok0 + pi * 512:tok0 + (pi + 1) * 512]
                        else:
                            dst = dest[ch, :, tok0 + pi * 512:tok0 + (pi + 1) * 512]
                        free["ob"][i3] = c.dma("sp", lambda e, i3=i3, dst=dst, M=M: e.dma_start(
                            out=dst, in_=ob[i3][0:M, :]), f"pob{i3}", deps=[t_o], final=True)
                w_rd[bf] = w_last
        tv = None
        if "tm" not in PROJ_DBG:
            c.run()
            return
        wvi = st.enter_context(_sb(nc, "pwvi", [128, KC, 128], BF16))
        wv2 = [st.enter_context(_sb(nc, f"pwv2{i}", [128, KC, 128], BF16)) for i in range(2)]
        for vi, o in enumerate((O_VA, O_VB)):
            tv = c.dma("pool", lambda e, o=o, vi=vi: e.dma_start(
                out=wv2[vi][:], in_=win[:, o:o + 128].rearrange("(k p) n -> p k n", p=128)), "pwv")
        tv = c.dma("pool", lambda e: e.dma_start(
            out=wvi[:], in_=win[:, O_KI:O_KI + 128].rearrange("(k p) n -> p k n", p=128)), "pwv")
        v_free = [None, None]
        v_st = [None, None]
        for tb in range(TT // 128):
            i = tb % 2
            for vi in range(2):
                for k in range(KC):
                    c.op("pe", lambda e, i=i, k=k, tb=tb, vi=vi: e.matmul(
                        vps[i][:, vi * 128:(vi + 1) * 128], u[:, k, tb * 128:(tb + 1) * 128], wv2[vi][:, k, :],
                        start=(k == 0), stop=(k == KC - 1)), deps=[tv, v_free[i]], sig=False)
            for k in range(KC):
                t_v = c.op("pe", lambda e, i=i, k=k, tb=tb: e.matmul(
                    vps[i][:, 256:272], u[:, k, tb * 128:(tb + 1) * 128], wvi[:, k, 64:80], start=(k == 0),
                    stop=(k == KC - 1)), deps=[tv, v_free[i]], sig=(k == KC - 1))
            t_c1 = c.op("act", lambda e, i=i: e.copy(vo[i][:, 0:256], vps[i][:, 0:256]), deps=[t_v, v_st[i]])
            t_c2 = c.op("dve", lambda e, i=i: e.tensor_scalar_mul(wo[i][:], vps[i][:, 256:272], 0.25),
                        deps=[t_v, v_st[i], t_c1])
            v_free[i] = t_c2
            r0 = tok0 + tb * 128
            if "novt" not in PROJ_DBG:
                c.dma("sp", lambda e, i=i, r0=r0: e.dma_start(out=g.vtok[b, r0:r0 + 128, :], in_=vo[i][:, 0:256]),
                      f"pvo{i}", deps=[t_c1], final=True)
            if "nowi" in PROJ_DBG:
                v_st[i] = None
                continue
            v_st[i] = c.dma("sp", lambda e, i=i, r0=r0: e.dma_start(out=g.witok[b, r0:r0 + 128, :], in_=wo[i][:]),
                            f"pvo{i}", deps=[t_c2, t_c1], final=True)
        c.run()


def blk_attn_a(nc, g, name, b):
    c = BlockCtx(nc, name)
    NBLK = S // 128
    with ExitStack() as st:
        qi = st.enter_context(_sb(nc, "aqi", [128, 8, S], BF16))
        ki = st.enter_context(_sb(nc, "aki", [128, S], BF16))
        qa = st.enter_context(_sb(nc, "aqa", [128, 8, S], BF16))
        ka = st.enter_context(_sb(nc, "aka", [128, S], BF16))
        va = st.enter_context(_sb(nc, "ava", [128, NBLK, 128], BF16))
        wi = st.enter_context(_sb(nc, "awi", [128, NBLK, 16], F32))
        acc = [st.enter_context(_sb(nc, f"aacc{i}", [128, S], F32)) for i in range(2)]
        wk = st.enter_context(_sb(nc, "awk", [128, S], F32))
        rl = [st.enter_context(_sb(nc, f"arl{i}", [128, 512], F32)) for i in range(2)]
        mx = st.enter_context(_sb(nc, "amx", [128, 8], F32))
        thr = st.enter_context(_sb(nc, "athr", [128, 1], F32))
        mk = st.enter_context(_sb(nc, "amk", [128, S], BF16))
        mkT = [st.enter_context(_sb(nc, f"amkT{i}", [128, NBLK, 128], BF16)) for i in range(2)]
        ex = [st.enter_context(_sb(nc, f"aex{i}", [128, 4, 128], BF16)) for i in range(2)]
        pm = [st.enter_context(_sb(nc, f"apm{i}", [128, 4, 128], BF16)) for i in range(2)]
        rc = st.enter_context(_sb(nc, "arc", [128, 512], F32))
        yo = [st.enter_context(_sb(nc, f"ayo{i}", [128, 4, 128], BF16)) for i in range(2)]
        dps = [st.enter_context(_ps(nc, f"adp{i}", [128, 512], F32)) for i in range(2)]
        lps = [st.enter_context(_ps(nc, f"alp{i}", [128, 512], F32)) for i in range(2)]
        ops_ = st.enter_context(_ps(nc, "aop", [128, 512], F32))
        nps = st.enter_context(_ps(nc, "anp", [128, 512], F32))
        tps = st.enter_context(_ps(nc, "atp", [128, 8, 128], BF16))
        ld = [
            c.dma("sp", lambda e: e.dma_start(out=qi[:], in_=g.qiT[b].rearrange("h p t -> p h t")), "ald"),
            c.dma("sp", lambda e: e.dma_start(out=ki[0:64, :], in_=g.kiT[b][0:64, :]), "ald"),
            c.dma("sp", lambda e: e.dma_start(out=ki[64:128, :], in_=g.kiT[b][0:64, :]), "ald"),
            c.dma("sp", lambda e: e.dma_start(out=qa[:], in_=g.qaT[b].rearrange("h p t -> p h t")), "ald"),
            c.dma("sp", lambda e: e.dma_start(out=ka[:], in_=g.kaT[b]), "ald"),
            c.dma("sp", lambda e: e.dma_start(
                out=va[:], in_=g.vtok[b, :, 0:128].rearrange("(n p) d -> p n d", p=128)), "ald"),
            c.dma("sp", lambda e: e.dma_start(
                out=wi[:], in_=g.witok[b].rearrange("(n p) h -> p n h", p=128)), "ald"),
        ]
        t_ld = ld[-1]
        dps_free = [None, None]
        rl_free = [None, None]
        acc_free = [None, None]
        mk_free = None
        mkT_free = [None, None]
        tps_free = None
        lps_free = [None, None]
        ex_free = [None, None]
        pm_free = [None, None]
        ops_free = None
        nps_free = None
        yo_st = [None, None]
        nd = 0
        nl = 0
        for i in range(NBLK):
            Lk = (i + 1) * 128
            ab = i % 2
            tq = slice(i * 128, (i + 1) * 128)
            t_acc = acc_free[ab]
            for sc in range((Lk + 511) // 512):
                w = min(512, Lk - sc * 512)
                ssl = slice(sc * 512, sc * 512 + w)
                for h in range(16):
                    d = nd % 2
                    nd += 1
                    po = (h % 2) * 64
                    t_d = c.op("pe", lambda e, d=d, po=po, h=h, tq=tq, ssl=ssl, w=w: e.matmul(
                        dps[d][:, 0:w], qi[po:po + 64, h // 2, tq], ki[po:po + 64, ssl], start=True, stop=True),
                        deps=[t_ld, dps_free[d]])
                    t_r = c.op("act", lambda e, d=d, w=w: e.activation(out=rl[d][:, 0:w], in_=dps[d][:, 0:w],
                                                                       func=AF.Relu), deps=[t_d, rl_free[d]])
                    dps_free[d] = t_r
                    if h == 0:
                        t_acc = c.op("dve", lambda e, d=d, ab=ab, ssl=ssl, w=w, i=i: e.tensor_scalar(
                            out=acc[ab][:, ssl], in0=rl[d][:, 0:w], scalar1=wi[:, i, 0:1], scalar2=None,
                            op0=ALU.mult), deps=[t_r, t_acc])
                    else:
                        t_acc = c.op("dve", lambda e, d=d, ab=ab, ssl=ssl, w=w, i=i, h=h: e.scalar_tensor_tensor(
                            out=acc[ab][:, ssl], in0=rl[d][:, 0:w], scalar=wi[:, i, h:h + 1], in1=acc[ab][:, ssl],
                            op0=ALU.mult, op1=ALU.add), deps=[t_r, t_acc])
                    rl_free[d] = t_acc
            t_acc = c.op("dve", lambda e, ab=ab, tq=tq: e.tensor_tensor(
                out=acc[ab][:, tq], in0=acc[ab][:, tq], in1=g.cbias[:], op=ALU.add), deps=[t_acc])
            if Lk > 256:
                src = acc[ab]
                t_m = t_acc
                for rnd in range(32):
                    t_m = c.op("dve", lambda e, src=src, Lk=Lk: e.max(out=mx[:], in_=src[:, 0:Lk]), deps=[t_m])
                    if rnd < 31:
                        t_m = c.op("dve", lambda e, src=src, Lk=Lk: e.match_replace(
                            out=wk[:, 0:Lk], in_to_replace=mx[:], in_values=src[:, 0:Lk], imm_value=NEG),
                            deps=[t_m])
                        src = wk
                t_th = c.op("dve", lambda e: e.tensor_scalar_max(thr[:], mx[:, 7:8], -1e29), deps=[t_m])
            else:
                t_th = c.op("dve", lambda e: e.memset(thr[:], -1e29), deps=[t_acc])
            t_mk = c.op("dve", lambda e, ab=ab, Lk=Lk: e.tensor_scalar(
                out=mk[:, 0:Lk], in0=acc[ab][:, 0:Lk], scalar1=thr[:, 0:1], scalar2=None, op0=ALU.is_ge),
                deps=[t_th, mk_free])
            acc_free[ab] = t_mk
            t_cp = mkT_free[ab]
            for s0 in range(0, i + 1, 8):
                ns = min(8, i + 1 - s0)
                for j in range(ns):
                    t_t = c.op("pe", lambda e, j=j, s0=s0: e.transpose(
                        tps[:, j, :], mk[:, (s0 + j) * 128:(s0 + j + 1) * 128], g.identB[:]),
                        deps=[t_mk, tps_free], sig=(j == ns - 1))
                t_cp = c.op("act", lambda e, ab=ab, s0=s0, ns=ns: e.copy(mkT[ab][:, s0:s0 + ns, :], tps[:, 0:ns, :]),
                            deps=[t_t, t_cp])
                tps_free = t_cp
            mk_free = t_t
            for hg in range(2):
                t_pv = None
                for sc in range(i + 1):
                    li = nl % 2
                    nl += 1
                    ks = slice(sc * 128, (sc + 1) * 128)
                    t_l = c.op("pe", lambda e, li=li, ks=ks, hg=hg, tq=tq: e.matmul(
                        lps[li][:], ka[:, ks], qa[:, hg * 4:(hg + 1) * 4, tq], start=True, stop=True),
                        deps=[t_ld, lps_free[li]])
                    t_e = c.op("act", lambda e, li=li: e.activation(
                        out=ex[li][:], in_=lps[li][:].rearrange("p (h t) -> p h t", h=4), func=AF.Exp),
                        deps=[t_l, ex_free[li]])
                    lps_free[li] = t_e
                    t_p = c.op("pool", lambda e, li=li, ab=ab, sc=sc: e.tensor_tensor(
                        out=pm[li][:], in0=ex[li][:], in1=bc(mkT[ab][:, sc, :], [128, 4, 128], 1), op=ALU.mult),
                        deps=[t_e, t_cp, pm_free[li]])
                    ex_free[li] = t_p
                    c.op("pe", lambda e, li=li, sc=sc, i=i: e.matmul(
                        ops_[:], va[:, sc, :], pm[li][:], start=(sc == 0), stop=(sc == i)),
                        deps=[t_p, ops_free if sc == 0 else None], sig=False)
                    t_pv = c.op("pe", lambda e, li=li, sc=sc, i=i: e.matmul(
                        nps[:], g.onesB[:], pm[li][:], start=(sc == 0), stop=(sc == i)),
                        deps=[t_p, nps_free if sc == 0 else None])
                    pm_free[li] = t_pv
                yb = (2 * i + hg) % 2
                t_rc = c.op("dve", lambda e: e.reciprocal(rc[:], nps[:]), deps=[t_pv])
                nps_free = t_rc
                t_y = c.op("dve", lambda e, yb=yb: e.tensor_tensor(
                    out=yo[yb][:], in0=ops_[:].rearrange("p (h t) -> p h t", h=4),
                    in1=rc[:].rearrange("p (h t) -> p h t", h=4), op=ALU.mult), deps=[t_rc, yo_st[yb]])
                ops_free = t_y
                yo_st[yb] = c.dma("sp", lambda e, yb=yb, hg=hg, tq=tq: e.dma_start(
                    out=g.yaT[b, hg * 4:(hg + 1) * 4, :, tq].rearrange("h p t -> p h t"), in_=yo[yb][:]),
                    f"ayo{yb}", deps=[t_y], final=True)
            mkT_free[ab] = t_pv
        c.run()


def blk_swa(nc, g, name, l, b):
    c = BlockCtx(nc, name)
    NBLK = S // 128
    with ExitStack() as st:
        qb = st.enter_context(_sb(nc, "bq", [128, 8, S], BF16))
        kb = st.enter_context(_sb(nc, "bk", [128, 2, S], BF16))
        vb = st.enter_context(_sb(nc, "bv", [128, NBLK, 2, 128], BF16))
        sk = st.enter_context(_sb(nc, "bsk", [128, 16], F32))
        ske = st.enter_context(_sb(nc, "bske", [128, 16], F32))
        ex = [st.enter_context(_sb(nc, f"bex{i}", [128, 8, 128], BF16)) for i in range(2)]
        pm = [st.enter_context(_sb(nc, f"bpm{i}", [128, 8, 128], BF16)) for i in range(2)]
        rc = st.enter_context(_sb(nc, "brc", [128, 8, 128], F32))
        yo = [st.enter_context(_sb(nc, f"byo{i}", [128, 8, 128], BF16)) for i in range(2)]
        lps = [st.enter_context(_ps(nc, f"blp{i}", [128, 1024], F32)) for i in range(2)]
        ops_ = st.enter_context(_ps(nc, "bop", [128, 1024], F32))
        nps = st.enter_context(_ps(nc, "bnp", [128, 1024], F32))
        ld = [c.dma("sp", lambda e: e.dma_start(out=qb[:], in_=g.qbT[b].rearrange("h p t -> p h t")), "bld")]
        for gq in range(2):
            for half in range(2):
                ld.append(c.dma("sp", lambda e, gq=gq, half=half: e.dma_start(
                    out=kb[half * 64:(half + 1) * 64, gq, :], in_=g.kbT[b][gq * 64:(gq + 1) * 64, :]), "bld"))
            for dup in range(2):
                ld.append(c.dma("sp", lambda e, gq=gq, dup=dup: e.dma_start(
                    out=vb[:, :, gq, dup * 64:(dup + 1) * 64],
                    in_=g.vtok[b, :, 128 + gq * 64:128 + (gq + 1) * 64].rearrange("(n p) d -> p n d", p=128)), "bld"))
        ld.append(c.dma("sp", lambda e: e.dma_start(
            out=sk[:], in_=g.w["sinks"][l:l + 1, :].to_broadcast([128, 16])), "bld"))
        t_ld = ld[-1]
        t_sk = c.op("act", lambda e: e.activation(out=ske[:], in_=sk[:], func=AF.Exp), deps=[t_ld])
        lps_free = [None, None]
        ex_free = [None, None]
        pm_free = [None, None]
        ops_free = None
        nps_free = None
        yo_st = [None, None]
        nl = 0
        ny = 0
        for i in range(min(NBLK, SWA_DBG)):
            tq = slice(i * 128, (i + 1) * 128)
            chunks = ([(i - 1, g.mprev)] if i > 0 else []) + [(i, g.mcur)]
            for gq in range(2):
                t_pv = None
                for ci, (sc, msk) in enumerate(chunks):
                    li = nl % 2
                    nl += 1
                    ks = slice(sc * 128, (sc + 1) * 128)
                    for r in range(8):
                        h = gq * 8 + r
                        po = (h % 2) * 64
                        cp = (r % 2) * 4 + r // 2
                        t_l = c.op("pe", lambda e, li=li, po=po, gq=gq, ks=ks, h=h, cp=cp, tq=tq: e.matmul(
                            lps[li][:, cp * 128:(cp + 1) * 128], kb[po:po + 64, gq, ks], qb[po:po + 64, h // 2, tq],
                            start=True, stop=True), deps=[t_ld, lps_free[li]], sig=(r == 7))
                    t_e = c.op("act", lambda e, li=li: e.activation(
                        out=ex[li][:], in_=lps[li][:].rearrange("p (h t) -> p h t", h=8), func=AF.Exp),
                        deps=[t_l, ex_free[li]])
                    lps_free[li] = t_e
                    t_p = c.op("pool", lambda e, li=li, msk=msk: e.tensor_tensor(
                        out=pm[li][:], in0=ex[li][:], in1=bc(msk[:], [128, 8, 128], 1), op=ALU.mult),
                        deps=[t_e, pm_free[li]])
                    ex_free[li] = t_p
                    first, last = (ci == 0), (ci == len(chunks) - 1)
                    for hf in range(2):
                        c.op("pe", lambda e, li=li, sc=sc, gq=gq, hf=hf, first=first, last=last: e.matmul(
                            ops_[:, hf * 512:(hf + 1) * 512], vb[:, sc, gq, :], pm[li][:, hf * 4:(hf + 1) * 4, :],
                            start=first, stop=last), deps=[t_p, ops_free if first else None], sig=False)
                    for hf in range(2):
                        t_pv = c.op("pe", lambda e, li=li, hf=hf, first=first, last=last: e.matmul(
                            nps[:, hf * 512:(hf + 1) * 512], g.onesB[:], pm[li][:, hf * 4:(hf + 1) * 4, :],
                            start=first, stop=last), deps=[t_p, nps_free if first else None], sig=(hf == 1))
                    pm_free[li] = t_pv
                t_dn = c.op("dve", lambda e, gq=gq: e.tensor_tensor(
                    out=rc[:].rearrange("p (two r2) t -> p two r2 t", two=2),
                    in0=nps[:].rearrange("p (two r2 t) -> p two r2 t", two=2, r2=4),
                    in1=ske[:, gq * 8:(gq + 1) * 8].rearrange("p (r2 two) -> p two r2", two=2).unsqueeze(3)
                    .to_broadcast([128, 2, 4, 128]), op=ALU.add), deps=[t_pv, t_sk])
                nps_free = t_dn
                t_rc = c.op("dve", lambda e: e.reciprocal(rc[:], rc[:]), deps=[t_dn])
                yb = ny % 2
                ny += 1
                t_y1 = c.op("dve", lambda e, yb=yb: e.tensor_tensor(
                    out=yo[yb][:], in0=ops_[:].rearrange("p (r t) -> p r t", r=8), in1=rc[:], op=ALU.mult),
                    deps=[t_rc, yo_st[yb]])
                ops_free = t_y1
                y4 = yo[yb][:].rearrange("p (two r2) t -> p two r2 t", two=2)
                c.dma("sp", lambda e, y4=y4, gq=gq, tq=tq: e.dma_start(
                    out=g.ybT[b, gq * 4:(gq + 1) * 4, 0:64, tq].rearrange("h p t -> p h t"), in_=y4[0:64, 0, :, :]),
                    f"byo{yb}", deps=[t_y1], final=True)
                yo_st[yb] = c.dma("sp", lambda e, y4=y4, gq=gq, tq=tq: e.dma_start(
                    out=g.ybT[b, gq * 4:(gq + 1) * 4, 64:128, tq].rearrange("h p t -> p h t"),
                    in_=y4[64:128, 1, :, :]), f"byo{yb}", deps=[t_y1], final=True)
        c.run()


def blk_merge(nc, g, name, l, b, tok0, u, merged):
    c = BlockCtx(nc, name)
    win = g.w["w_in"][l]
    NP = TT // 512
    with ExitStack() as st:
        ya = st.enter_context(_sb(nc, "mya", [128, 8, TT], BF16))
        yb_ = st.enter_context(_sb(nc, "myb", [128, 8, TT], BF16))
        woa = [st.enter_context(_sb(nc, f"mwoa{i}", [128, 8, 256], BF16)) for i in range(2)]
        wob = [st.enter_context(_sb(nc, f"mwob{i}", [128, 8, 256], BF16)) for i in range(2)]
        wga = [st.enter_context(_sb(nc, f"mwga{i}", [128, KC, 256], BF16)) for i in range(2)]
        wgb = [st.enter_context(_sb(nc, f"mwgb{i}", [128, KC, 256], BF16)) for i in range(2)]
        sga = [st.enter_context(_sb(nc, f"msga{i}", [128, 512], F32)) for i in range(2)]
        sgb = [st.enter_context(_sb(nc, f"msgb{i}", [128, 512], F32)) for i in range(2)]
        aps_ = [st.enter_context(_ps(nc, f"mpa{i}", [128, 512], F32)) for i in range(2)]
        bps_ = [st.enter_context(_ps(nc, f"mpb{i}", [128, 512], F32)) for i in range(2)]
        gaps = [st.enter_context(_ps(nc, f"mpga{i}", [128, 512], F32)) for i in range(2)]
        gbps = [st.enter_context(_ps(nc, f"mpgb{i}", [128, 512], F32)) for i in range(2)]
        t_ld = c.dma("sp", lambda e: e.dma_start(
            out=ya[:], in_=g.yaT[b, :, :, tok0:tok0 + TT].rearrange("h p t -> p h t")), "mld")
        t_ld = c.dma("sp", lambda e: e.dma_start(
            out=yb_[:], in_=g.ybT[b, :, :, tok0:tok0 + TT].rearrange("h p t -> p h t")), "mld")
        w_rd = [None, None]
        fr = {"a": [None, None], "b": [None, None], "ga": [None, None], "gb": [None, None]}
        sg_free = [None, None]
        n = 0
        t_w = None
        for nch in range(KC):
            stg, col = nch // 2, (nch % 2) * 128
            bf = stg % 2
            if nch % 2 == 0:
                cs = slice(stg * 256, (stg + 1) * 256)
                for (dst, src, ck) in ((woa, g.w["w_o_a"][l][:, cs], "k"), (wob, g.w["w_o_b"][l][:, cs], "k"),
                                       (wga, win[:, O_GA + stg * 256:O_GA + (stg + 1) * 256], "k"),
                                       (wgb, win[:, O_GB + stg * 256:O_GB + (stg + 1) * 256], "k")):
                    t_w = c.dma("pool", lambda e, dst=dst, src=src, bf=bf: e.dma_start(
                        out=dst[bf][:], in_=src.rearrange("(k p) n -> p k n", p=128)), f"mw{bf}", deps=[w_rd[bf]])
            for pi in range(NP):
                i = n % 2
                n += 1
                tsl = slice(pi * 512, (pi + 1) * 512)
                for k in range(8):
                    t_a = c.op("pe", lambda e, i=i, bf=bf, k=k, col=col, tsl=tsl: e.matmul(
                        aps_[i][:], woa[bf][:, k, col:col + 128], ya[:, k, tsl], start=(k == 0), stop=(k == 7)),
                        deps=[t_w, t_ld, fr["a"][i]], sig=(k == 7))
                for k in range(8):
                    t_b = c.op("pe", lambda e, i=i, bf=bf, k=k, col=col, tsl=tsl: e.matmul(
                        bps_[i][:], wob[bf][:, k, col:col + 128], yb_[:, k, tsl], start=(k == 0), stop=(k == 7)),
                        deps=[t_w, t_ld, fr["b"][i]], sig=(k == 7))
                for k in range(KC):
                    t_ga = c.op("pe", lambda e, i=i, bf=bf, k=k, col=col, tsl=tsl: e.matmul(
                        gaps[i][:], wga[bf][:, k, col:col + 128], u[:, k, tsl], start=(k == 0), stop=(k == KC - 1)),
                        deps=[t_w, fr["ga"][i]], sig=(k == KC - 1))
                for k in range(KC):
                    t_gb = c.op("pe", lambda e, i=i, bf=bf, k=k, col=col, tsl=tsl: e.matmul(
                        gbps[i][:], wgb[bf][:, k, col:col + 128], u[:, k, tsl], start=(k == 0), stop=(k == KC - 1)),
                        deps=[t_w, fr["gb"][i]], sig=(k == KC - 1))
                t_last = t_gb
                t_s1 = c.op("act", lambda e, i=i: e.activation(out=sga[i][:], in_=gaps[i][:], func=AF.Sigmoid),
                            deps=[t_ga, sg_free[i]])
                t_s2 = c.op("act", lambda e, i=i: e.activation(out=sgb[i][:], in_=gbps[i][:], func=AF.Sigmoid),
                            deps=[t_gb, sg_free[i]])
                fr["ga"][i] = t_s1
                fr["gb"][i] = t_s2
                t_m1 = c.op("dve", lambda e, i=i: e.tensor_tensor(out=sga[i][:], in0=sga[i][:], in1=aps_[i][:],
                                                                  op=ALU.mult), deps=[t_s1, t_a])
                fr["a"][i] = t_m1
                t_m2 = c.op("dve", lambda e, i=i: e.tensor_tensor(out=sgb[i][:], in0=sgb[i][:], in1=bps_[i][:],
                                                                  op=ALU.mult), deps=[t_s2, t_b])
                fr["b"][i] = t_m2
                t_m3 = c.op("pool", lambda e, i=i, nch=nch, tsl=tsl: e.tensor_tensor(
                    out=merged[:, nch, tsl], in0=sga[i][:], in1=sgb[i][:], op=ALU.add), deps=[t_m1, t_m2])
                sg_free[i] = t_m3
            if nch % 2 == 1:
                w_rd[bf] = t_last
        c.run()


WEIGHT_NAMES = ["ada_w", "ada_b", "norm_ffn1_g", "ffn1_w_gate", "ffn1_w_up", "ffn1_w_down", "norm_mix_g",
                "w_in", "qn_a_g", "kn_a_g", "qn_b_g", "kn_b_g", "sinks", "w_o_a", "w_o_b", "w_out",
                "norm_ffn2_g", "ffn2_w_gate", "ffn2_w_up", "ffn2_w_down"]
WEIGHT_SHAPES = {
    "ada_w": [L, D, 9 * D], "ada_b": [L, 9 * D], "norm_ffn1_g": [L, D], "ffn1_w_gate": [L, D, FF],
    "ffn1_w_up": [L, D, FF], "ffn1_w_down": [L, FF, D], "norm_mix_g": [L, D], "w_in": [L, D, N_IN],
    "qn_a_g": [L, 128], "kn_a_g": [L, 128], "qn_b_g": [L, 64], "kn_b_g": [L, 64], "sinks": [L, 16],
    "w_o_a": [L, 1024, D], "w_o_b": [L, 1024, D], "w_out": [L, D, D], "norm_ffn2_g": [L, D],
    "ffn2_w_gate": [L, D, FF], "ffn2_w_up": [L, D, FF], "ffn2_w_down": [L, FF, D]}


FFN_PARTS = ("n", "g", "d")
DBG_TILES = 99
NWB = 2
DBG_NSEQ = NSEQ


def ffn_sublayer(nc, g, l, which, b, u, hidden):
    j = 0 if which == 1 else 2
    wg = g.w[f"ffn{which}_w_gate"][l]
    wu = g.w[f"ffn{which}_w_up"][l]
    wd = g.w[f"ffn{which}_w_down"][l]
    for tt in range(min(S // TT, DBG_TILES)):
        tok0 = tt * TT
        nm = f"l{l}f{which}b{b}t{tt}"
        if "n" in FFN_PARTS:
            blk_norm(nc, g, nm + "n", b, j, tok0, TT, u)
        if "g" in FFN_PARTS:
            blk_gu(nc, g, nm + "g", wg, wu, u, hidden, TT)
        if "d" in FFN_PARTS:
            blk_down(nc, g, nm + "d", wd, hidden, b, j, tok0, TT)


SWA_DBG = 99
MIX_PARTS = ("p", "a", "b", "m")
PROJ_DBG = ("qa", "ka", "qi", "ki", "qb", "kb", "tm")


def mix_sublayer(nc, g, l, b, u):
    nm = f"l{l}mb{b}"
    if "p" in MIX_PARTS:
        for hf in range(S // TT):
            blk_norm(nc, g, f"{nm}n{hf}", b, 1, hf * TT, TT, u)
            blk_proj(nc, g, f"{nm}p{hf}", l, b, hf * TT, u)
    if "a" in MIX_PARTS:
        blk_attn_a(nc, g, nm + "a", b)
    if "b" in MIX_PARTS:
        blk_swa(nc, g, nm + "s", l, b)
    if "m" in MIX_PARTS:
        with _sb(nc, "merged", [128, KC, TT], BF16) as merged:
            for hf in range(S // TT):
                blk_norm(nc, g, f"{nm}q{hf}", b, 1, hf * TT, TT, u)
                blk_merge(nc, g, f"{nm}m{hf}", l, b, hf * TT, u, merged)
                blk_down(nc, g, f"{nm}o{hf}", g.w["w_out"][l], merged, b, 1, hf * TT, TT, nfc=KC)


def build_program(n_layers=L, stages=("ffn1", "mix", "ffn2"), parts=("init", "pre", "post")):
    nc = bass.Bass("TRN2", target_bir_lowering=False)
    g = G()
    g.x_in = nc.dram_tensor("x", [NSEQ, S, D], F32, kind="ExternalInput").ap()
    g.c_in = nc.dram_tensor("c", [NSEQ, D], F32, kind="ExternalInput").ap()
    g.pos_in = nc.dram_tensor("positions", [NSEQ, S], I32, kind="ExternalInput").ap()
    g.consts_in = nc.dram_tensor("consts", [128, 8], F32, kind="ExternalInput").ap()
    g.w = {n: nc.dram_tensor(n, WEIGHT_SHAPES[n], F32, kind="ExternalInput").ap() for n in WEIGHT_NAMES}
    g.ada_w, g.ada_b = g.w["ada_w"], g.w["ada_b"]
    g.norm_g = [g.w["norm_ffn1_g"], g.w["norm_mix_g"], g.w["norm_ffn2_g"]]
    g.out = nc.dram_tensor("out", [NSEQ, S, D], F32, kind="ExternalOutput").ap()
    g.hT = nc.dram_tensor("hT", [NSEQ, D, S], F32).ap()
    g.rope = nc.dram_tensor("rope", [NSEQ, 6, 128, S], F32).ap()
    for nm_ in ("qaT", "qiT", "qbT", "yaT", "ybT"):
        setattr(g, nm_, nc.dram_tensor(nm_, [NSEQ, 8, 128, S], BF16).ap())
    for nm_ in ("kaT", "kiT", "kbT"):
        setattr(g, nm_, nc.dram_tensor(nm_, [NSEQ, 128, S], BF16).ap())
    g.vtok = nc.dram_tensor("vtok", [NSEQ, S, 256], BF16).ap()
    g.witok = nc.dram_tensor("witok", [NSEQ, S, 16], F32).ap()
    with ExitStack() as st:
        def sb(name, shape, dt):
            return st.enter_context(nc.sbuf_tensor(name, shape, dt))
        g.identF = sb("identF", [128, 128], F32)
        g.identB = sb("identB", [128, 128], BF16)
        g.onesF = sb("onesF", [128, 128], F32)
        g.epsc = sb("epsc", [128, 1], F32)
        g.cactT = sb("cactT", [128, KC, NSEQ], BF16)
        g.mod = sb("mod", [128, 9 * KC, NSEQ], F32)
        g.normg = sb("normg", [128, 3 * KC], F32)
        g.gsc = sb("gsc", [128, 3, KC, NSEQ], F32)
        g.gt = sb("gt", [128, 3, KC, NSEQ], F32)
        g.permA = sb("permA", [128, 128], F32)
        g.permB = sb("permB", [128, 128], F32)
        g.permI = sb("permI", [128, 128], F32)
        g.onesBD = sb("onesBD", [128, 128], F32)
        g.onesB = sb("onesB", [128, 128], BF16)
        g.cbias = sb("cbias", [128, 128], F32)
        g.mcurF = sb("mcurF", [128, 128], F32)
        g.mprevF = sb("mprevF", [128, 128], F32)
        g.mcur = sb("mcur", [128, 128], BF16)
        g.mprev = sb("mprev", [128, 128], BF16)
        g.cols = sb("cols", [128, 8], F32)
        u = sb("u", [128, KC, TT], BF16)
        if "init" in parts:
            blk_init(nc, g)
            blk_init2(nc, g)
        for b in range(NSEQ):
            if "pre" in parts:
                blk_pre(nc, g, b)
            if "mix" in stages and b < DBG_NSEQ:
                blk_rope(nc, g, b)
        for l in range(n_layers):
            blk_mod(nc, g, l)
            for b in range(DBG_NSEQ):
                if "ffn1" in stages:
                    with _sb(nc, "hidden", [128, FC, TT], BF16) as hidden:
                        ffn_sublayer(nc, g, l, 1, b, u, hidden)
                if "mix" in stages:
                    mix_sublayer(nc, g, l, b, u)
                if "ffn2" in stages:
                    with _sb(nc, "hidden", [128, FC, TT], BF16) as hidden:
                        ffn_sublayer(nc, g, l, 2, b, u, hidden)
        for b in range(NSEQ):
            if "post" in parts:
                blk_post(nc, g, b)
    return nc


def kernel(**inputs):
    n = 8
    nc = build_program()
    in_maps = []
    x = np.ascontiguousarray(inputs["x"], dtype=np.float32)
    c = np.ascontiguousarray(inputs["c"], dtype=np.float32)
    pos = np.ascontiguousarray(inputs["positions"], dtype=np.int32)
    ws = {k: np.ascontiguousarray(inputs[k], dtype=np.float32) for k in WEIGHT_NAMES}
    consts = host_consts()
    for i in range(n):
        m = {"x": x[NSEQ * i:NSEQ * (i + 1)], "c": c[NSEQ * i:NSEQ * (i + 1)],
             "positions": pos[NSEQ * i:NSEQ * (i + 1)], "consts": consts}
        m.update(ws)
        in_maps.append(m)
    res = run_bass_kernel_spmd(nc, in_maps, core_ids=list(range(n)))
    return np.concatenate([r["out"] for r in res.results], axis=0)
```

```python
import math
from contextlib import ExitStack

import numpy as np
import concourse.bass as bass
import concourse.mybir as mybir
from concourse.bass_utils import run_bass_kernel_spmd

F32, BF16, I32 = mybir.dt.float32, mybir.dt.bfloat16, mybir.dt.int32
ALU = mybir.AluOpType
AF = mybir.ActivationFunctionType

D = 2048
S = 2048
FF = 5632
L = 4
KC = D // 128
FC = FF // 128
NSEQ = 2
N_IN = 7760
EPS = 1e-6
TT = 1024

O_QA, O_KA, O_VA, O_QI, O_KI, O_WI, O_QB, O_KB, O_VB, O_GA, O_GB = (
    0, 1024, 1152, 1280, 2304, 2368, 2384, 3408, 3536, 3664, 5712)


_SEMS = {}
_CNT = {}


class BlockCtx:
    def __init__(self, nc, name):
        self.nc = nc
        self.name = name
        self.ops = {e: [] for e in ("pe", "act", "dve", "pool", "sp")}
        self.cnt = _CNT.setdefault(id(nc), {})
        self.final = []

    def _t(self, key, inc):
        self.cnt[key] = self.cnt.get(key, 0) + inc
        return (key, self.cnt[key])

    def op(self, eng, fn, deps=(), sig=True):
        t = self._t(eng, 1) if sig else None
        self.ops[eng].append((fn, tuple(d for d in deps if d is not None), t, 1))
        return t

    def dma(self, eng, fn, key, deps=(), final=False):
        t = self._t("q" + key, 16)
        self.ops[eng].append((fn, tuple(d for d in deps if d is not None), t, 16))
        if final:
            self.final.append(t)
        return t

    def run(self):
        nc = self.nc
        if self.final:
            mx = {}
            for k, v in self.final:
                mx[k] = max(mx.get(k, 0), v)
            self.ops["sp"].append((None, tuple(mx.items()), None, 0))
        pool = _SEMS.setdefault(id(nc), {})
        for k in self.cnt:
            if k not in pool:
                pool[k] = nc.alloc_semaphore(f"s_{k}")
        sems = pool
        with ExitStack() as st:
            blk = st.enter_context(nc.Block())
            decos = {"pe": blk.tensor, "act": blk.scalar, "dve": blk.vector,
                     "pool": blk.gpsimd, "sp": blk.sync}
            for eng, lst in self.ops.items():
                if not lst:
                    continue

                def body(e, lst=lst):
                    waited = {}
                    for fn, deps, t, inc in lst:
                        for k, v in deps:
                            if waited.get(k, 0) < v:
                                e.wait_ge(sems[k], v)
                                waited[k] = v
                        if fn is None:
                            continue
                        ins = fn(e)
                        if t is not None:
                            ins.then_inc(sems[t[0]], inc)

                decos[eng](body)


_UID = [0]


def _sb(nc, name, shape, dt):
    _UID[0] += 1
    return nc.sbuf_tensor(f"{name}_{_UID[0]}", shape, dt)


def _ps(nc, name, shape, dt):
    _UID[0] += 1
    return nc.psum_tensor(f"{name}_{_UID[0]}", shape, dt)


class G:
    pass


def bc(ap, shape, axis):
    return ap.unsqueeze(axis).to_broadcast(list(shape))


def blk_init(nc, g):
    c = BlockCtx(nc, "init")
    with _sb(nc, "crow", [32, 128], F32) as crow, \
            _ps(nc, "cps", [128, 32], F32) as cps:
        t1 = c.op("pool", lambda e: e.memset(g.identF[:], 0.0))
        t2 = c.op("pool", lambda e: e.affine_select(
            out=g.identF[:], in_=g.identF[:], compare_op=ALU.not_equal, fill=1.0, base=0,
            pattern=[[-1, 128]], channel_multiplier=1), deps=[t1])
        c.op("pool", lambda e: e.memset(g.onesF[:], 1.0))
        c.op("pool", lambda e: e.memset(g.epsc[:], EPS))
        c.op("dve", lambda e: e.tensor_copy(g.identB[:], g.identF[:]), deps=[t2])
        tl = c.dma("sp", lambda e: e.dma_start(
            out=crow[:], in_=g.c_in.rearrange("b (k p) -> (b k) p", p=128)), "crow")
        tt = c.op("pe", lambda e: e.transpose(cps[:], crow[:], g.identF[0:32, 0:32]), deps=[tl, t2])
        c.op("act", lambda e: e.activation(
            out=g.cactT[:].rearrange("p k b -> p b k"),
            in_=cps[:].rearrange("p (b k) -> p b k", b=NSEQ), func=AF.Silu), deps=[tt])
        c.run()


def blk_pre(nc, g, b):
    c = BlockCtx(nc, f"pre{b}")
    with ExitStack() as st:
        xin = [st.enter_context(_sb(nc, f"xin{i}", [128, 4, D], F32)) for i in range(2)]
        xo = [st.enter_context(_sb(nc, f"xo{i}", [128, KC, 512], F32)) for i in range(2)]
        tp = [st.enter_context(_ps(nc, f"tp{i}", [128, 512], F32)) for i in range(4)]
        xin_rd = [None, None]
        xo_st = [None, None]
        tp_free = [None] * 4
        n = 0
        for i in range(S // 512):
            bf = i % 2
            tok0 = i * 512
            t_ld = c.dma("sp", lambda e, bf=bf, tok0=tok0: e.dma_start(
                out=xin[bf][:], in_=g.x_in[b, tok0:tok0 + 512, :].rearrange("(j p) d -> p j d", p=128)),
                f"xin{bf}", deps=[xin_rd[bf]])
            cps = []
            for k in range(KC):
                pb = n % 4
                n += 1
                for j in range(4):
                    t_tr = c.op("pe", lambda e, pb=pb, j=j, k=k, bf=bf: e.transpose(
                        tp[pb][:, j * 128:(j + 1) * 128], xin[bf][:, j, k * 128:(k + 1) * 128], g.identF[:]),
                        deps=[t_ld, tp_free[pb]], sig=(j == 3))
                eng = "act" if k % 2 == 0 else "dve"
                if eng == "act":
                    t_cp = c.op("act", lambda e, pb=pb, k=k, bf=bf: e.copy(xo[bf][:, k, :], tp[pb][:]),
                                deps=[t_tr, xo_st[bf]])
                else:
                    t_cp = c.op("dve", lambda e, pb=pb, k=k, bf=bf: e.tensor_copy(xo[bf][:, k, :], tp[pb][:]),
                                deps=[t_tr, xo_st[bf]])
                tp_free[pb] = t_cp
                cps.append(t_cp)
            xin_rd[bf] = t_tr
            xo_st[bf] = c.dma("sp", lambda e, bf=bf, tok0=tok0: e.dma_start(
                out=g.hT[b, :, tok0:tok0 + 512].rearrange("(k p) t -> p k t", p=128), in_=xo[bf][:]),
                f"xo{bf}", deps=cps[-2:], final=True)
        c.run()


def blk_post(nc, g, b):
    c = BlockCtx(nc, f"post{b}")
    with ExitStack() as st:
        hin = [st.enter_context(_sb(nc, f"hin{i}", [128, KC, 512], F32)) for i in range(2)]
        yo = [st.enter_context(_sb(nc, f"yo{i}", [128, 4, D], F32)) for i in range(2)]
        tp = [st.enter_context(_ps(nc, f"tq{i}", [128, 512], F32)) for i in range(4)]
        hin_rd = [None, None]
        yo_st = [None, None]
        tp_free = [None] * 4
        n = 0
        for i in range(S // 512):
            bf = i % 2
            tok0 = i * 512
            t_ld = c.dma("sp", lambda e, bf=bf, tok0=tok0: e.dma_start(
                out=hin[bf][:], in_=g.hT[b, :, tok0:tok0 + 512].rearrange("(k p) t -> p k t", p=128)),
                f"hin{bf}", deps=[hin_rd[bf]])
            cps = []
            for j in range(4):
                for kg in range(4):
                    pb = n % 4
                    n += 1
                    for kk in range(4):
                        k = kg * 4 + kk
                        t_tr = c.op("pe", lambda e, pb=pb, j=j, k=k, kk=kk, bf=bf: e.transpose(
                            tp[pb][:, kk * 128:(kk + 1) * 128], hin[bf][:, k, j * 128:(j + 1) * 128],
                            g.identF[:]), deps=[t_ld, tp_free[pb]], sig=(kk == 3))
                    if n % 2 == 0:
                        t_cp = c.op("act", lambda e, pb=pb, j=j, kg=kg, bf=bf: e.copy(
                            yo[bf][:, j, kg * 512:(kg + 1) * 512], tp[pb][:]), deps=[t_tr, yo_st[bf]])
                    else:
                        t_cp = c.op("dve", lambda e, pb=pb, j=j, kg=kg, bf=bf: e.tensor_copy(
                            yo[bf][:, j, kg * 512:(kg + 1) * 512], tp[pb][:]), deps=[t_tr, yo_st[bf]])
                    tp_free[pb] = t_cp
                    cps.append(t_cp)
            hin_rd[bf] = t_tr
            yo_st[bf] = c.dma("sp", lambda e, bf=bf, tok0=tok0: e.dma_start(
                out=g.out[b, tok0:tok0 + 512, :].rearrange("(j p) d -> p j d", p=128), in_=yo[bf][:]),
                f"yo{bf}", deps=cps[-2:], final=True)
        c.run()


def blk_mod(nc, g, l):
    c = BlockCtx(nc, f"mod{l}")
    NTILE = 9 * D // 512
    with ExitStack() as st:
        rowsA = st.enter_context(_sb(nc, "rowsA", [128, 128], F32))
        rowsB = st.enter_context(_sb(nc, "rowsB", [64, 128], F32))
        abT = st.enter_context(_sb(nc, "abT", [128, 144], F32))
        wt = [st.enter_context(_sb(nc, f"adaw{i}", [128, KC, 512], BF16)) for i in range(3)]
        rps = st.enter_context(_ps(nc, "rps", [128, 192], F32))
        mps = st.enter_context(_ps(nc, "mps", [128, 288], F32))
        ab = g.ada_b[l].rearrange("(r p) -> r p", p=128)
        lds = [
            c.dma("sp", lambda e: e.dma_start(out=rowsA[:], in_=ab[0:128, :]), "rows"),
            c.dma("sp", lambda e: e.dma_start(out=rowsB[0:16, :], in_=ab[128:144, :]), "rows"),
            c.dma("sp", lambda e: e.dma_start(
                out=rowsB[16:32, :], in_=g.norm_g[0][l].rearrange("(r p) -> r p", p=128)), "rows"),
            c.dma("sp", lambda e: e.dma_start(
                out=rowsB[32:48, :], in_=g.norm_g[1][l].rearrange("(r p) -> r p", p=128)), "rows"),
            c.dma("sp", lambda e: e.dma_start(
                out=rowsB[48:64, :], in_=g.norm_g[2][l].rearrange("(r p) -> r p", p=128)), "rows"),
        ]
        tA = c.op("pe", lambda e: e.transpose(rps[:, 0:128], rowsA[:], g.identF[:]), deps=[lds[-1]], sig=False)
        tB = c.op("pe", lambda e: e.transpose(rps[:, 128:192], rowsB[:], g.identF[0:64, 0:64]), deps=[lds[-1]])
        t_ab = c.op("dve", lambda e: e.tensor_copy(abT[:], rps[:, 0:144]), deps=[tB])
        t_ng = c.op("dve", lambda e: e.tensor_copy(g.normg[:], rps[:, 144:192]), deps=[tB])
        w_rd = [None] * 3
        t_mm = None
        for ti in range(NTILE):
            bf = ti % 3
            t_w = c.dma("pool", lambda e, bf=bf, ti=ti: e.dma_start(
                out=wt[bf][:], in_=g.ada_w[l, :, ti * 512:(ti + 1) * 512].rearrange("(k p) n -> p k n", p=128)),
                f"adaw{bf}", deps=[w_rd[bf]])
            for n in range(4):
                nch = ti * 4 + n
                for k in range(KC):
                    t_mm = c.op("pe", lambda e, bf=bf, n=n, k=k, nch=nch: e.matmul(
                        mps[:, nch * 2:nch * 2 + 2], wt[bf][:, k, n * 128:(n + 1) * 128], g.cactT[:, k, :],
                        start=(k == 0), stop=(k == KC - 1)), deps=[t_w], sig=(n == 3 and k == KC - 1))
            w_rd[bf] = t_mm
        t_mod = c.op("dve", lambda e: e.tensor_tensor(
            out=g.mod[:], in0=mps[:].rearrange("p (m b) -> p m b", b=NSEQ),
            in1=bc(abT[:], [128, 144, NSEQ], 2), op=ALU.add), deps=[t_mm, t_ab])
        for j in range(3):
            c.op("dve", lambda e, j=j: e.scalar_tensor_tensor(
                out=g.gsc[:, j], in0=g.mod[:, (3 * j + 1) * KC:(3 * j + 2) * KC, :], scalar=1.0,
                in1=bc(g.normg[:, j * KC:(j + 1) * KC], [128, KC, NSEQ], 2), op0=ALU.add, op1=ALU.mult),
                deps=[t_mod, t_ng])
            c.op("dve", lambda e, j=j: e.tensor_scalar_mul(
                g.gt[:, j], g.mod[:, (3 * j + 2) * KC:(3 * j + 3) * KC, :], 1.0 if j == 1 else 0.5),
                deps=[t_mod])
        c.run()


def blk_norm(nc, g, name, b, j, tok0, ntok, u):
    c = BlockCtx(nc, name)
    P = 256
    with ExitStack() as st:
        hs = [st.enter_context(_sb(nc, f"hs{i}", [128, KC, P], F32)) for i in range(2)]
        sq = [st.enter_context(_sb(nc, f"sq{i}", [128, KC, P], F32)) for i in range(2)]
        rstd = [st.enter_context(_sb(nc, f"rstd{i}", [128, P], F32)) for i in range(2)]
        ssp = [st.enter_context(_ps(nc, f"ssp{i}", [128, P], F32)) for i in range(2)]
        hs_free = [None, None]
        sq_free = [None, None]
        rstd_free = [None, None]
        ssp_free = [None, None]
        for i in range(ntok // P):
            bf = i % 2
            t0 = tok0 + i * P
            t_ld = c.dma("sp", lambda e, bf=bf, t0=t0: e.dma_start(
                out=hs[bf][:], in_=g.hT[b, :, t0:t0 + P].rearrange("(k p) t -> p k t", p=128)),
                f"hs{bf}", deps=[hs_free[bf]])
            t_sq = c.op("act", lambda e, bf=bf: e.activation(out=sq[bf][:], in_=hs[bf][:], func=AF.Square),
                        deps=[t_ld, sq_free[bf]])
            for k in range(KC):
                t_ss = c.op("pe", lambda e, bf=bf, k=k: e.matmul(
                    ssp[bf][:], g.onesF[:], sq[bf][:, k, :], start=(k == 0), stop=(k == KC - 1)),
                    deps=[t_sq, ssp_free[bf]], sig=(k == KC - 1))
            sq_free[bf] = t_ss
            t_r1 = c.op("act", lambda e, bf=bf: e.activation(
                out=rstd[bf][:], in_=ssp[bf][:], func=AF.Sqrt, scale=1.0 / D, bias=g.epsc[:]),
                deps=[t_ss, rstd_free[bf]])
            ssp_free[bf] = t_r1
            t_r2 = c.op("dve", lambda e, bf=bf: e.reciprocal(rstd[bf][:], rstd[bf][:]), deps=[t_r1])
            t_a = c.op("dve", lambda e, bf=bf: e.tensor_tensor(
                out=hs[bf][:], in0=hs[bf][:], in1=bc(rstd[bf][:], [128, KC, P], 1), op=ALU.mult),
                deps=[t_r2, t_sq])
            t_b = c.op("pool", lambda e, bf=bf: e.tensor_tensor(
                out=hs[bf][:], in0=hs[bf][:], in1=bc(g.gsc[:, j, :, b], [128, KC, P], 2), op=ALU.mult),
                deps=[t_a])
            rstd_free[bf] = t_a
            t_c = c.op("dve", lambda e, bf=bf, i=i: e.tensor_tensor(
                out=u[:, :, i * P:(i + 1) * P], in0=hs[bf][:],
                in1=bc(g.mod[:, 3 * j * KC:(3 * j + 1) * KC, b], [128, KC, P], 2), op=ALU.add),
                deps=[t_b])
            hs_free[bf] = t_c
        c.run()


def blk_gu(nc, g, name, wg_ap, wu_ap, u, hidden, ntok):
    c = BlockCtx(nc, name)
    NH = ntok // 512
    with ExitStack() as st:
        wg = [st.enter_context(_sb(nc, f"wg{i}", [128, KC, 256], BF16)) for i in range(NWB)]
        wu = [st.enter_context(_sb(nc, f"wu{i}", [128, KC, 256], BF16)) for i in range(NWB)]
        sg = [st.enter_context(_sb(nc, f"sg{i}", [128, ntok], F32)) for i in range(2)]
        gps = [st.enter_context(_ps(nc, f"gps{i}", [128, ntok], F32)) for i in range(2)]
        ups = [st.enter_context(_ps(nc, f"ups{i}", [128, ntok], F32)) for i in range(2)]
        w_rd = [None] * NWB
        t_silu = [None, None]
        t_mul = [None, None]
        t_w = None
        for fc in range(FC):
            stg, col, pp = fc // 2, (fc % 2) * 128, fc % 2
            bf = stg % NWB
            if fc % 2 == 0:
                c.dma("pool", lambda e, bf=bf, stg=stg: e.dma_start(
                    out=wg[bf][:], in_=wg_ap[:, stg * 256:(stg + 1) * 256].rearrange("(k p) f -> p k f", p=128)),
                    f"w{bf}", deps=[w_rd[bf]])
                t_w = c.dma("pool", lambda e, bf=bf, stg=stg: e.dma_start(
                    out=wu[bf][:], in_=wu_ap[:, stg * 256:(stg + 1) * 256].rearrange("(k p) f -> p k f", p=128)),
                    f"w{bf}", deps=[w_rd[bf]])
            for h in range(NH):
                for k in range(KC):
                    t_g = c.op("pe", lambda e, pp=pp, h=h, k=k, bf=bf, col=col: e.matmul(
                        gps[pp][:, h * 512:(h + 1) * 512], wg[bf][:, k, col:col + 128],
                        u[:, k, h * 512:(h + 1) * 512], start=(k == 0), stop=(k == KC - 1)),
                        deps=[t_w, t_silu[pp]], sig=(h == NH - 1 and k == KC - 1))
            for h in range(NH):
                for k in range(KC):
                    t_u = c.op("pe", lambda e, pp=pp, h=h, k=k, bf=bf, col=col: e.matmul(
                        ups[pp][:, h * 512:(h + 1) * 512], wu[bf][:, k, col:col + 128],
                        u[:, k, h * 512:(h + 1) * 512], start=(k == 0), stop=(k == KC - 1)),
                        deps=[t_w, t_mul[pp]], sig=(h == NH - 1 and k == KC - 1))
            if fc % 2 == 1:
                w_rd[bf] = t_u
            t_silu[pp] = c.op("act", lambda e, pp=pp: e.activation(out=sg[pp][:], in_=gps[pp][:], func=AF.Silu),
                              deps=[t_g, t_mul[pp]])
            t_mul[pp] = c.op("dve", lambda e, pp=pp, fc=fc: e.tensor_tensor(
                out=hidden[:, fc, 0:ntok], in0=sg[pp][:], in1=ups[pp][:], op=ALU.mult),
                deps=[t_silu[pp], t_u])
        c.run()


def blk_down(nc, g, name, wd_ap, hidden, b, j, tok0, ntok, nfc=FC):
    c = BlockCtx(nc, name)
    NH = ntok // 512
    with ExitStack() as st:
        wd = [st.enter_context(_sb(nc, f"wd{i}", [128, nfc, 256], BF16)) for i in range(2)]
        qsz = 11 if nfc % 11 == 0 else 8
        hr = [st.enter_context(_sb(nc, f"hr{i}", [128, ntok], F32)) for i in range(3)]
        dps = [st.enter_context(_ps(nc, f"dps{i}", [128, ntok], F32)) for i in range(2)]
        w_rd = [None, None]
        hr_st = [None] * 3
        dps_free = [None, None]
        t_w = None
        for n in range(KC):
            stg, col, pp = n // 2, (n % 2) * 128, n % 2
            bf = stg % 2
            rb = n % 3
            if n % 2 == 0:
                for q in range(nfc // qsz):
                    t_w = c.dma("pool", lambda e, bf=bf, stg=stg, q=q: e.dma_start(
                        out=wd[bf][:, q * qsz:(q + 1) * qsz, :],
                        in_=wd_ap[q * qsz * 128:(q + 1) * qsz * 128, stg * 256:(stg + 1) * 256].rearrange(
                            "(f p) n -> p f n", p=128)),
                        f"wd{bf}", deps=[w_rd[bf]])
            t_h = c.dma("sp", lambda e, rb=rb, n=n: e.dma_start(
                out=hr[rb][:], in_=g.hT[b, n * 128:(n + 1) * 128, tok0:tok0 + ntok]),
                f"hr{rb}", deps=[hr_st[rb]])
            for h in range(NH):
                for f in range(nfc):
                    t_d = c.op("pe", lambda e, pp=pp, h=h, f=f, bf=bf, col=col: e.matmul(
                        dps[pp][:, h * 512:(h + 1) * 512], wd[bf][:, f, col:col + 128],
                        hidden[:, f, h * 512:(h + 1) * 512], start=(f == 0), stop=(f == nfc - 1)),
                        deps=[t_w, dps_free[pp]], sig=(h == NH - 1 and f == nfc - 1))
            if n % 2 == 1:
                w_rd[bf] = t_d
            t_r = c.op("dve", lambda e, pp=pp, rb=rb, n=n: e.scalar_tensor_tensor(
                out=hr[rb][:], in0=dps[pp][:], scalar=g.gt[:, j, n, b:b + 1], in1=hr[rb][:],
                op0=ALU.mult, op1=ALU.add), deps=[t_d, t_h])
            dps_free[pp] = t_r
            hr_st[rb] = c.dma("sp", lambda e, rb=rb, n=n: e.dma_start(
                out=g.hT[b, n * 128:(n + 1) * 128, tok0:tok0 + ntok], in_=hr[rb][:]),
                f"hr{rb}", deps=[t_r], final=True)
        c.run()


NEG = -1e30
TWO_PI = 2.0 * math.pi
C1 = 6.28125
C2 = TWO_PI - C1


def host_consts():
    d = np.arange(128)
    th = 10000.0
    cols = np.zeros((128, 8), np.float32)
    cols[:, 0] = (np.float32(th) ** (-(2.0 * (d % 64)).astype(np.float32) / np.float32(128))).astype(np.float32)
    cols[:, 1] = (np.float32(th) ** (-(2.0 * (d % 32)).astype(np.float32) / np.float32(64))).astype(np.float32)
    fi = (np.float32(th) ** (-(2.0 * (d % 16)).astype(np.float32) / np.float32(32))).astype(np.float32)
    cols[:, 2] = np.where((d % 64) < 32, fi, 0.0)
    cols[:, 3] = np.where(d < 64, -1.0, 1.0)
    cols[:, 4] = np.where((d % 64) < 32, -1.0, 1.0)
    cols[:, 5] = np.where((d % 64) < 16, -1.0, 1.0)
    return cols


def blk_init2(nc, g):
    c = BlockCtx(nc, "init2")

    def sel(tile_ap, base, cmp=ALU.not_equal, fill=1.0):
        return lambda e: e.affine_select(out=tile_ap, in_=tile_ap, compare_op=cmp, fill=fill, base=base,
                                         pattern=[[-1, tile_ap.shape[1]]], channel_multiplier=1)
    t = c.op("pool", lambda e: e.memset(g.permA[:], 0.0))
    t = c.op("pool", sel(g.permA[:], -64), deps=[t])
    t = c.op("pool", sel(g.permA[:], 64), deps=[t])
    t = c.op("pool", lambda e: e.memset(g.permB[:], 0.0))
    for jb in range(4):
        base = -(32 * jb) - 32 if jb % 2 == 0 else -(32 * jb) + 32
        t = c.op("pool", sel(g.permB[:, 32 * jb:32 * jb + 32], base), deps=[t])
    t = c.op("pool", lambda e: e.memset(g.permI[:], 0.0))
    for jb in range(8):
        if jb % 4 == 0:
            t = c.op("pool", sel(g.permI[:, 16 * jb:16 * jb + 16], -(16 * jb) - 16), deps=[t])
        elif jb % 4 == 1:
            t = c.op("pool", sel(g.permI[:, 16 * jb:16 * jb + 16], -(16 * jb) + 16), deps=[t])
    t = c.op("pool", lambda e: e.memset(g.onesBD[:], 0.0))
    t = c.op("pool", lambda e: e.memset(g.onesBD[0:64, 0:64], 1.0), deps=[t])
    t = c.op("pool", lambda e: e.memset(g.onesBD[64:128, 64:128], 1.0), deps=[t])
    c.op("pool", lambda e: e.memset(g.onesB[:], 1.0))
    t = c.op("pool", lambda e: e.memset(g.cbias[:], 0.0))
    t = c.op("pool", sel(g.cbias[:], 0, ALU.is_ge, NEG), deps=[t])
    t = c.op("pool", lambda e: e.memset(g.mcurF[:], 1.0))
    t = c.op("pool", lambda e: e.affine_select(out=g.mcurF[:], in_=g.mcurF[:], compare_op=ALU.is_ge, fill=0.0,
                                               base=0, pattern=[[1, 128]], channel_multiplier=-1), deps=[t])
    c.op("dve", lambda e: e.tensor_copy(g.mcur[:], g.mcurF[:]), deps=[t])
    t = c.op("pool", lambda e: e.memset(g.mprevF[:], 1.0))
    t = c.op("pool", lambda e: e.affine_select(out=g.mprevF[:], in_=g.mprevF[:], compare_op=ALU.is_gt, fill=0.0,
                                               base=0, pattern=[[-1, 128]], channel_multiplier=1), deps=[t])
    c.op("dve", lambda e: e.tensor_copy(g.mprev[:], g.mprevF[:]), deps=[t])
    c.dma("sp", lambda e: e.dma_start(out=g.cols[:], in_=g.consts_in), "cols", final=True)
    c.run()


def blk_rope(nc, g, b):
    c = BlockCtx(nc, f"rope{b}")
    with ExitStack() as st:
        posi = st.enter_context(_sb(nc, "posi", [128, S], I32))
        posf = st.enter_context(_sb(nc, "posf", [128, S], F32))
        ang = st.enter_context(_sb(nc, "ang", [128, S], F32))
        ki = st.enter_context(_sb(nc, "ki", [128, S], I32))
        kf = st.enter_context(_sb(nc, "kf", [128, S], F32))
        r = st.enter_context(_sb(nc, "r", [128, S], F32))
        m = st.enter_context(_sb(nc, "m", [128, S], F32))
        res = [st.enter_context(_sb(nc, f"res{i}", [128, S], F32)) for i in range(2)]
        tl = c.dma("sp", lambda e: e.dma_start(out=posi[:], in_=g.pos_in[b:b + 1, :].to_broadcast([128, S])), "pos")
        t = c.op("dve", lambda e: e.tensor_copy(posf[:], posi[:]), deps=[tl])
        res_st = [None, None]
        n = 0
        for ty in range(3):
            for which in range(2):
                rb = n % 2
                shift = math.pi / 2 if which == 0 else 0.0
                t = c.op("dve", lambda e, ty=ty, shift=shift: e.tensor_scalar(
                    out=ang[:], in0=posf[:], scalar1=g.cols[:, ty:ty + 1], scalar2=shift,
                    op0=ALU.mult, op1=ALU.add), deps=[t])
                t = c.op("dve", lambda e: e.tensor_scalar(
                    out=ki[:], in0=ang[:], scalar1=1.0 / TWO_PI, scalar2=None, op0=ALU.mult), deps=[t])
                t = c.op("dve", lambda e: e.tensor_copy(kf[:], ki[:]), deps=[t])
                t = c.op("dve", lambda e: e.scalar_tensor_tensor(
                    out=r[:], in0=kf[:], scalar=-C1, in1=ang[:], op0=ALU.mult, op1=ALU.add), deps=[t])
                t = c.op("dve", lambda e: e.scalar_tensor_tensor(
                    out=r[:], in0=kf[:], scalar=-C2, in1=r[:], op0=ALU.mult, op1=ALU.add), deps=[t])
                t = c.op("dve", lambda e: e.tensor_scalar(
                    out=m[:], in0=r[:], scalar1=math.pi, scalar2=-TWO_PI, op0=ALU.is_gt, op1=ALU.mult), deps=[t])
                t = c.op("dve", lambda e: e.tensor_tensor(out=r[:], in0=r[:], in1=m[:], op=ALU.add), deps=[t])
                t = c.op("dve", lambda e: e.tensor_scalar(
                    out=m[:], in0=r[:], scalar1=-math.pi, scalar2=TWO_PI, op0=ALU.is_lt, op1=ALU.mult), deps=[t])
                t = c.op("dve", lambda e: e.tensor_tensor(out=r[:], in0=r[:], in1=m[:], op=ALU.add), deps=[t])
                t = c.op("dve", lambda e: e.tensor_scalar(
                    out=r[:], in0=r[:], scalar1=math.pi, scalar2=-math.pi, op0=ALU.min, op1=ALU.max), deps=[t])
                ta = c.op("act", lambda e, rb=rb: e.activation(out=res[rb][:], in_=r[:], func=AF.Sin),
                          deps=[t, res_st[rb]])
                t = ta
                if which == 1:
                    ta = c.op("dve", lambda e, rb=rb, ty=ty: e.tensor_scalar(
                        out=res[rb][:], in0=res[rb][:], scalar1=g.cols[:, 3 + ty:4 + ty], scalar2=None,
                        op0=ALU.mult), deps=[ta])
                    t = ta
                res_st[rb] = c.dma("sp", lambda e, rb=rb, ty=ty, which=which: e.dma_start(
                    out=g.rope[b, 2 * ty + which], in_=res[rb][:]), f"res{rb}", deps=[ta], final=True)
                n += 1
        c.run()


def blk_proj(nc, g, name, l, b, tok0, u):
    c = BlockCtx(nc, name)
    win = g.w["w_in"][l]
    NP = TT // 512
    groups = [("qa", O_QA, 8, 0, 128, g.qaT[b]), ("ka", O_KA, 1, 0, 128, g.kaT[b]),
              ("qi", O_QI, 8, 2, 0, g.qiT[b]), ("ki", O_KI, 1, 2, 0, g.kiT[b]),
              ("qb", O_QB, 8, 1, 64, g.qbT[b]), ("kb", O_KB, 1, 1, 64, g.kbT[b])]
    with ExitStack() as st:
        wt = [st.enter_context(_sb(nc, f"pw{i}", [128, KC, 512], BF16)) for i in range(2)]
        wv = st.enter_context(_sb(nc, "pwv", [128, KC, 272], BF16))
        gcol = st.enter_context(_sb(nc, "gcol", [128, 4], F32))
        grow = st.enter_context(_sb(nc, "grow", [128, 4], F32))
        tabs = [st.enter_context(_sb(nc, f"tab{i}", [128, 2, TT], F32)) for i in range(3)]
        sq = [st.enter_context(_sb(nc, f"psq{i}", [128, 512], F32)) for i in range(2)]
        xn = [st.enter_context(_sb(nc, f"pxn{i}", [128, 512], F32)) for i in range(2)]
        rs = [st.enter_context(_sb(nc, f"prs{i}", [128, 512], F32)) for i in range(2)]
        t1 = [st.enter_context(_sb(nc, f"pt1{i}", [128, 512], F32)) for i in range(2)]
        ob = [st.enter_context(_sb(nc, f"pob{i}", [128, 512], BF16)) for i in range(3)]
        vo = [st.enter_context(_sb(nc, f"pvo{i}", [128, 272], BF16)) for i in range(2)]
        wo = [st.enter_context(_sb(nc, f"pwo{i}", [128, 16], F32)) for i in range(2)]
        pp = [st.enter_context(_ps(nc, f"ppp{i}", [128, 512], F32)) for i in range(2)]
        ssp = [st.enter_context(_ps(nc, f"pss{i}", [128, 512], F32)) for i in range(2)]
        swp = [st.enter_context(_ps(nc, f"psw{i}", [128, 512], F32)) for i in range(2)]
        vps = [st.enter_context(_ps(nc, f"pvp{i}", [128, 512], F32)) for i in range(2)]
        growT = st.enter_context(_sb(nc, "growT", [4, 128], F32))
        gl = [
            c.dma("sp", lambda e: e.dma_start(out=growT[0:1, :], in_=g.w["qn_a_g"][l:l + 1, :]), "gr"),
            c.dma("sp", lambda e: e.dma_start(out=growT[1:2, :], in_=g.w["kn_a_g"][l:l + 1, :]), "gr"),
            c.dma("sp", lambda e: e.dma_start(out=growT[2:3, 0:64], in_=g.w["qn_b_g"][l:l + 1, :]), "gr"),
            c.dma("sp", lambda e: e.dma_start(out=growT[2:3, 64:128], in_=g.w["qn_b_g"][l:l + 1, :]), "gr"),
            c.dma("sp", lambda e: e.dma_start(out=growT[3:4, 0:64], in_=g.w["kn_b_g"][l:l + 1, :]), "gr"),
            c.dma("sp", lambda e: e.dma_start(out=growT[3:4, 64:128], in_=g.w["kn_b_g"][l:l + 1, :]), "gr"),
        ]
        t_gt = c.op("pe", lambda e: e.transpose(pp[0][:, 0:4], growT[:], g.identF[0:4, 0:4]), deps=[gl[-1]])
        t_gc = c.op("dve", lambda e: e.tensor_copy(grow[:], pp[0][:, 0:4]), deps=[t_gt])
        tg = c.op("dve", lambda e: e.tensor_scalar_mul(gcol[:, 0:1], grow[:, 0:1], 128.0 ** -0.5), deps=[t_gc])
        tg = c.op("dve", lambda e: e.tensor_copy(gcol[:, 1:2], grow[:, 1:2]), deps=[t_gc])
        tg = c.op("dve", lambda e: e.tensor_scalar_mul(gcol[:, 2:3], grow[:, 2:3], 64.0 ** -0.5), deps=[t_gc])
        tg = c.op("dve", lambda e: e.tensor_copy(gcol[:, 3:4], grow[:, 3:4]), deps=[t_gc])
        ttab = None
        for ty in range(3):
            for which in range(2):
                ttab = c.dma("sp", lambda e, ty=ty, which=which: e.dma_start(
                    out=tabs[ty][:, which, :], in_=g.rope[b, 2 * ty + which, :, tok0:tok0 + TT]), "tab")
        w_rd = [None, None]
        cnt = {"w": 0, "u": 0}
        free = {"pp": [t_gc, None], "ssp": [None, None], "swp": [None, None], "sq": [None, None],
                "xn": [None, None], "rs": [None, None], "t1": [None, None], "ob": [None] * 3}
        for (gname, col0, nch, ty, hd, dest) in groups:
            if gname not in PROJ_DBG:
                continue
            gi = {"qa": 0, "ka": 1, "qb": 2, "kb": 3}.get(gname, None)
            for c0 in range(0, nch, 4):
                ncc = min(4, nch - c0)
                bf = cnt["w"] % 2
                cnt["w"] += 1
                width = ncc * 128 if gname != "ki" else 64
                t_w = c.dma("pool", lambda e, bf=bf, col0=col0, c0=c0, width=width: e.dma_start(
                    out=wt[bf][:, :, 0:width],
                    in_=win[:, col0 + c0 * 128:col0 + c0 * 128 + width].rearrange("(k p) n -> p k n", p=128)),
                    f"pw{bf}", deps=[w_rd[bf]])
                for cc in range(ncc):
                    ch = c0 + cc
                    M = 128 if gname != "ki" else 64
                    for pi in range(NP):
                        i = cnt["u"] % 2
                        i3 = cnt["u"] % 3
                        cnt["u"] += 1
                        tsl = slice(pi * 512, (pi + 1) * 512)
                        for k in range(KC):
                            t_p = c.op("pe", lambda e, i=i, bf=bf, cc=cc, k=k, tsl=tsl, M=M: e.matmul(
                                pp[i][0:M, :], wt[bf][:, k, cc * 128:cc * 128 + M], u[:, k, tsl],
                                start=(k == 0), stop=(k == KC - 1)),
                                deps=[t_w, free["pp"][i]], sig=(k == KC - 1))
                        w_last = t_p
                        if hd:
                            t_sq = c.op("act", lambda e, i=i: e.activation(out=sq[i][:], in_=pp[i][:], func=AF.Square),
                                        deps=[t_p, free["sq"][i]])
                            ones = g.onesF if hd == 128 else g.onesBD
                            t_ss = c.op("pe", lambda e, i=i, ones=ones: e.matmul(
                                ssp[i][:], ones[:], sq[i][:], start=True, stop=True), deps=[t_sq, free["ssp"][i]])
                            free["sq"][i] = t_ss
                            t_r = c.op("act", lambda e, i=i, hd=hd: e.activation(
                                out=rs[i][:], in_=ssp[i][:], func=AF.Sqrt, scale=1.0 / hd, bias=g.epsc[:]),
                                deps=[t_ss, free["rs"][i]])
                            free["ssp"][i] = t_r
                            t_r = c.op("dve", lambda e, i=i: e.reciprocal(rs[i][:], rs[i][:]), deps=[t_r])
                            t_x = c.op("dve", lambda e, i=i, gi=gi: e.scalar_tensor_tensor(
                                out=xn[i][:], in0=pp[i][:], scalar=gcol[:, gi:gi + 1], in1=rs[i][:],
                                op0=ALU.mult, op1=ALU.mult), deps=[t_r, tg, free["xn"][i]])
                            free["rs"][i] = t_x
                        else:
                            sc_ = 0.125 if gname == "qi" else 1.0
                            t_x = c.op("act", lambda e, i=i, sc_=sc_, M=M: e.activation(
                                out=xn[i][0:M, :], in_=pp[i][0:M, :], func=AF.Copy, scale=sc_),
                                deps=[t_p, free["xn"][i]])
                        free["pp"][i] = t_x
                        perm = [g.permA, g.permB, g.permI][ty]
                        t_s = c.op("pe", lambda e, i=i, perm=perm, M=M: e.matmul(
                            swp[i][0:M, :], perm[0:M, 0:M], xn[i][0:M, :], start=True, stop=True),
                            deps=[t_x, free["swp"][i]])
                        t_a = c.op("pool", lambda e, i=i, ty=ty, tsl=tsl, M=M: e.tensor_tensor(
                            out=t1[i][0:M, :], in0=xn[i][0:M, :], in1=tabs[ty][0:M, 0, tsl], op=ALU.mult),
                            deps=[t_x, ttab, free["t1"][i]])
                        t_b = c.op("dve", lambda e, i=i, ty=ty, tsl=tsl, M=M: e.tensor_tensor(
                            out=xn[i][0:M, :], in0=swp[i][0:M, :], in1=tabs[ty][0:M, 1, tsl], op=ALU.mult),
                            deps=[t_s, ttab, t_a])
                        free["swp"][i] = t_b
                        t_o = c.op("dve", lambda e, i=i, i3=i3, M=M: e.tensor_tensor(
                            out=ob[i3][0:M, :], in0=t1[i][0:M, :], in1=xn[i][0:M, :], op=ALU.add),
                            deps=[t_b, free["ob"][i3]])
                        free["t1"][i] = t_o
                        free["xn"][i] = t_o
                        if gname == "ki":
                            dst = dest[0:64, tok0 + pi * 512:tok0 + (pi + 1) * 512]
                        elif nch == 1:
                            dst = dest[:, tok0 + pi * 512:tok0 + (pi + 1) * 512]
                        else:
                            dst = dest[ch, :, tok0 + pi * 512:tok0 + (pi + 1) * 512]
                        free["ob"][i3] = c.dma("sp", lambda e, i3=i3, dst=dst, M=M: e.dma_start(
                            out=dst, in_=ob[i3][0:M, :]), f"pob{i3}", deps=[t_o], final=True)
                w_rd[bf] = w_last
        tv = None
        if "tm" not in PROJ_DBG:
            c.run()
            return
        wvi = st.enter_context(_sb(nc, "pwvi", [128, KC, 128], BF16))
        wv2 = [st.enter_context(_sb(nc, f"pwv2{i}", [128, KC, 128], BF16)) for i in range(2)]
        for vi, o in enumerate((O_VA, O_VB)):
            tv = c.dma("pool", lambda e, o=o, vi=vi: e.dma_start(
                out=wv2[vi][:], in_=win[:, o:o + 128].rearrange("(k p) n -> p k n", p=128)), "pwv")
        tv = c.dma("pool", lambda e: e.dma_start(
            out=wvi[:], in_=win[:, O_KI:O_KI + 128].rearrange("(k p) n -> p k n", p=128)), "pwv")
        v_free = [None, None]
        v_st = [None, None]
        for tb in range(TT // 128):
            i = tb % 2
            for vi in range(2):
                for k in range(KC):
                    c.op("pe", lambda e, i=i, k=k, tb=tb, vi=vi: e.matmul(
                        vps[i][:, vi * 128:(vi + 1) * 128], u[:, k, tb * 128:(tb + 1) * 128], wv2[vi][:, k, :],
                        start=(k == 0), stop=(k == KC - 1)), deps=[tv, v_free[i]], sig=False)
            for k in range(KC):
                t_v = c.op("pe", lambda e, i=i, k=k, tb=tb: e.matmul(
                    vps[i][:, 256:272], u[:, k, tb * 128:(tb + 1) * 128], wvi[:, k, 64:80], start=(k == 0),
                    stop=(k == KC - 1)), deps=[tv, v_free[i]], sig=(k == KC - 1))
            t_c1 = c.op("act", lambda e, i=i: e.copy(vo[i][:, 0:256], vps[i][:, 0:256]), deps=[t_v, v_st[i]])
            t_c2 = c.op("dve", lambda e, i=i: e.tensor_scalar_mul(wo[i][:], vps[i][:, 256:272], 0.25),
                        deps=[t_v, v_st[i], t_c1])
            v_free[i] = t_c2
            r0 = tok0 + tb * 128
            if "novt" not in PROJ_DBG:
                c.dma("sp", lambda e, i=i, r0=r0: e.dma_start(out=g.vtok[b, r0:r0 + 128, :], in_=vo[i][:, 0:256]),
                      f"pvo{i}", deps=[t_c1], final=True)
            if "nowi" in PROJ_DBG:
                v_st[i] = None
                continue
            v_st[i] = c.dma("sp", lambda e, i=i, r0=r0: e.dma_start(out=g.witok[b, r0:r0 + 128, :], in_=wo[i][:]),
                            f"pvo{i}", deps=[t_c2, t_c1], final=True)
        c.run()


def blk_attn_a(nc, g, name, b):
    c = BlockCtx(nc, name)
    NBLK = S // 128
    with ExitStack() as st:
        qi = st.enter_context(_sb(nc, "aqi", [128, 8, S], BF16))
        ki = st.enter_context(_sb(nc, "aki", [128, S], BF16))
        qa = st.enter_context(_sb(nc, "aqa", [128, 8, S], BF16))
        ka = st.enter_context(_sb(nc, "aka", [128, S], BF16))
        va = st.enter_context(_sb(nc, "ava", [128, NBLK, 128], BF16))
        wi = st.enter_context(_sb(nc, "awi", [128, NBLK, 16], F32))
        acc = [st.enter_context(_sb(nc, f"aacc{i}", [128, S], F32)) for i in range(2)]
        wk = st.enter_context(_sb(nc, "awk", [128, S], F32))
        rl = [st.enter_context(_sb(nc, f"arl{i}", [128, 512], F32)) for i in range(2)]
        mx = st.enter_context(_sb(nc, "amx", [128, 8], F32))
        thr = st.enter_context(_sb(nc, "athr", [128, 1], F32))
        mk = [st.enter_context(_sb(nc, f"amk{i}", [128, S], BF16)) for i in range(2)]
        mkT = [st.enter_context(_sb(nc, f"amkT{i}", [128, NBLK, 128], BF16)) for i in range(2)]
        ex = [st.enter_context(_sb(nc, f"aex{i}", [128, 4, 128], BF16)) for i in range(2)]
        pm = [st.enter_context(_sb(nc, f"apm{i}", [128, 4, 128], BF16)) for i in range(2)]
        rc = st.enter_context(_sb(nc, "arc", [128, 512], F32))
        yo = [st.enter_context(_sb(nc, f"ayo{i}", [128, 4, 128], BF16)) for i in range(2)]
        dps = [st.enter_context(_ps(nc, f"adp{i}", [128, 512], F32)) for i in range(2)]
        lps = [st.enter_context(_ps(nc, f"alp{i}", [128, 512], F32)) for i in range(2)]
        ops_ = st.enter_context(_ps(nc, "aop", [128, 512], F32))
        nps = st.enter_context(_ps(nc, "anp", [128, 512], F32))
        tps = st.enter_context(_ps(nc, "atp", [128, 8, 128], BF16))
        ld = [
            c.dma("sp", lambda e: e.dma_start(out=qi[:], in_=g.qiT[b].rearrange("h p t -> p h t")), "ald"),
            c.dma("sp", lambda e: e.dma_start(out=ki[0:64, :], in_=g.kiT[b][0:64, :]), "ald"),
            c.dma("sp", lambda e: e.dma_start(out=ki[64:128, :], in_=g.kiT[b][0:64, :]), "ald"),
            c.dma("sp", lambda e: e.dma_start(out=qa[:], in_=g.qaT[b].rearrange("h p t -> p h t")), "ald"),
            c.dma("sp", lambda e: e.dma_start(out=ka[:], in_=g.kaT[b]), "ald"),
            c.dma("sp", lambda e: e.dma_start(
                out=va[:], in_=g.vtok[b, :, 0:128].rearrange("(n p) d -> p n d", p=128)), "ald"),
            c.dma("sp", lambda e: e.dma_start(
                out=wi[:], in_=g.witok[b].rearrange("(n p) h -> p n h", p=128)), "ald"),
        ]
        t_ld = ld[-1]
        dps_free = [None, None]
        rl_free = [None, None]
        acc_free = [None, None]
        mk_free = [None, None]
        mkT_free = [None, None]
        tps_free = None
        lps_free = [None, None]
        ex_free = [None, None]
        pm_free = [None, None]
        ops_free = None
        nps_free = None
        yo_st = [None, None]
        nd = 0
        nl = 0
        def phase_a(i):
            nonlocal nd
            Lk = (i + 1) * 128
            ab = i % 2
            tq = slice(i * 128, (i + 1) * 128)
            t_acc = acc_free[ab]
            for sc in range((Lk + 511) // 512):
                w = min(512, Lk - sc * 512)
                ssl = slice(sc * 512, sc * 512 + w)
                for h in range(16):
                    d = nd % 2
                    nd += 1
                    po = (h % 2) * 64
                    t_d = c.op("pe", lambda e, d=d, po=po, h=h, tq=tq, ssl=ssl, w=w: e.matmul(
                        dps[d][:, 0:w], qi[po:po + 64, h // 2, tq], ki[po:po + 64, ssl], start=True, stop=True),
                        deps=[t_ld, dps_free[d]])
                    t_r = c.op("act", lambda e, d=d, w=w: e.activation(out=rl[d][:, 0:w], in_=dps[d][:, 0:w],
                                                                       func=AF.Relu), deps=[t_d, rl_free[d]])
                    dps_free[d] = t_r
                    if h == 0:
                        t_acc = c.op("dve", lambda e, d=d, ab=ab, ssl=ssl, w=w, i=i: e.tensor_scalar(
                            out=acc[ab][:, ssl], in0=rl[d][:, 0:w], scalar1=wi[:, i, 0:1], scalar2=None,
                            op0=ALU.mult), deps=[t_r, t_acc])
                    else:
                        t_acc = c.op("dve", lambda e, d=d, ab=ab, ssl=ssl, w=w, i=i, h=h: e.scalar_tensor_tensor(
                            out=acc[ab][:, ssl], in0=rl[d][:, 0:w], scalar=wi[:, i, h:h + 1], in1=acc[ab][:, ssl],
                            op0=ALU.mult, op1=ALU.add), deps=[t_r, t_acc])
                    rl_free[d] = t_acc
            t_acc = c.op("dve", lambda e, ab=ab, tq=tq: e.tensor_tensor(
                out=acc[ab][:, tq], in0=acc[ab][:, tq], in1=g.cbias[:], op=ALU.add), deps=[t_acc])
            if Lk > 256:
                src = acc[ab]
                t_m = t_acc
                for rnd in range(32):
                    t_m = c.op("dve", lambda e, src=src, Lk=Lk: e.max(out=mx[:], in_=src[:, 0:Lk]), deps=[t_m])
                    if rnd < 31:
                        t_m = c.op("dve", lambda e, src=src, Lk=Lk: e.match_replace(
                            out=wk[:, 0:Lk], in_to_replace=mx[:], in_values=src[:, 0:Lk], imm_value=NEG),
                            deps=[t_m])
                        src = wk
                t_th = c.op("dve", lambda e: e.tensor_scalar_max(thr[:], mx[:, 7:8], -1e29), deps=[t_m])
            else:
                t_th = c.op("dve", lambda e: e.memset(thr[:], -1e29), deps=[t_acc])
            t_mk = c.op("dve", lambda e, ab=ab, Lk=Lk: e.tensor_scalar(
                out=mk[ab][:, 0:Lk], in0=acc[ab][:, 0:Lk], scalar1=thr[:, 0:1], scalar2=None, op0=ALU.is_ge),
                deps=[t_th, mk_free[ab]])
            acc_free[ab] = t_mk
            return t_mk

        def phase_b(i, t_mk):
            nonlocal nl, tps_free, ops_free, nps_free
            Lk = (i + 1) * 128
            ab = i % 2
            tq = slice(i * 128, (i + 1) * 128)
            t_cp = mkT_free[ab]
            for s0 in range(0, i + 1, 8):
                ns = min(8, i + 1 - s0)
                for j in range(ns):
                    t_t = c.op("pe", lambda e, j=j, s0=s0: e.transpose(
                        tps[:, j, :], mk[ab][:, (s0 + j) * 128:(s0 + j + 1) * 128], g.identB[:]),
                        deps=[t_mk, tps_free], sig=(j == ns - 1))
                t_cp = c.op("act", lambda e, ab=ab, s0=s0, ns=ns: e.copy(mkT[ab][:, s0:s0 + ns, :], tps[:, 0:ns, :]),
                            deps=[t_t, t_cp])
                tps_free = t_cp
            mk_free[ab] = t_t
            for hg in range(2):
                t_pv = None
                for sc in range(i + 1):
                    li = nl % 2
                    nl += 1
                    ks = slice(sc * 128, (sc + 1) * 128)
                    t_l = c.op("pe", lambda e, li=li, ks=ks, hg=hg, tq=tq: e.matmul(
                        lps[li][:], ka[:, ks], qa[:, hg * 4:(hg + 1) * 4, tq], start=True, stop=True),
                        deps=[t_ld, lps_free[li]])
                    t_e = c.op("act", lambda e, li=li: e.activation(
                        out=ex[li][:], in_=lps[li][:].rearrange("p (h t) -> p h t", h=4), func=AF.Exp),
                        deps=[t_l, ex_free[li]])
                    lps_free[li] = t_e
                    t_p = c.op("pool", lambda e, li=li, ab=ab, sc=sc: e.tensor_tensor(
                        out=pm[li][:], in0=ex[li][:], in1=bc(mkT[ab][:, sc, :], [128, 4, 128], 1), op=ALU.mult),
                        deps=[t_e, t_cp, pm_free[li]])
                    ex_free[li] = t_p
                    c.op("pe", lambda e, li=li, sc=sc, i=i: e.matmul(
                        ops_[:], va[:, sc, :], pm[li][:], start=(sc == 0), stop=(sc == i)),
                        deps=[t_p, ops_free if sc == 0 else None], sig=False)
                    t_pv = c.op("pe", lambda e, li=li, sc=sc, i=i: e.matmul(
                        nps[:], g.onesB[:], pm[li][:], start=(sc == 0), stop=(sc == i)),
                        deps=[t_p, nps_free if sc == 0 else None])
                    pm_free[li] = t_pv
                yb = (2 * i + hg) % 2
                t_rc = c.op("dve", lambda e: e.reciprocal(rc[:], nps[:]), deps=[t_pv])
                nps_free = t_rc
                t_y = c.op("dve", lambda e, yb=yb: e.tensor_tensor(
                    out=yo[yb][:], in0=ops_[:].rearrange("p (h t) -> p h t", h=4),
                    in1=rc[:].rearrange("p (h t) -> p h t", h=4), op=ALU.mult), deps=[t_rc, yo_st[yb]])
                ops_free = t_y
                yo_st[yb] = c.dma("sp", lambda e, yb=yb, hg=hg, tq=tq: e.dma_start(
                    out=g.yaT[b, hg * 4:(hg + 1) * 4, :, tq].rearrange("h p t -> p h t"), in_=yo[yb][:]),
                    f"ayo{yb}", deps=[t_y], final=True)
            mkT_free[ab] = t_pv

        t_mks = {0: phase_a(0)}
        for i in range(1, NBLK):
            t_mks[i] = phase_a(i)
            phase_b(i - 1, t_mks[i - 1])
        phase_b(NBLK - 1, t_mks[NBLK - 1])
        c.run()


def blk_swa(nc, g, name, l, b):
    c = BlockCtx(nc, name)
    NBLK = S // 128
    with ExitStack() as st:
        qb = st.enter_context(_sb(nc, "bq", [128, 8, S], BF16))
        kb = st.enter_context(_sb(nc, "bk", [128, 2, S], BF16))
        vb = st.enter_context(_sb(nc, "bv", [128, NBLK, 2, 128], BF16))
        sk = st.enter_context(_sb(nc, "bsk", [128, 16], F32))
        ske = st.enter_context(_sb(nc, "bske", [128, 16], F32))
        ex = [st.enter_context(_sb(nc, f"bex{i}", [128, 8, 128], BF16)) for i in range(2)]
        pm = [st.enter_context(_sb(nc, f"bpm{i}", [128, 8, 128], BF16)) for i in range(2)]
        rc = st.enter_context(_sb(nc, "brc", [128, 8, 128], F32))
        yo = [st.enter_context(_sb(nc, f"byo{i}", [128, 8, 128], BF16)) for i in range(2)]
        lps = [st.enter_context(_ps(nc, f"blp{i}", [128, 1024], F32)) for i in range(2)]
        ops_ = st.enter_context(_ps(nc, "bop", [128, 1024], F32))
        nps = st.enter_context(_ps(nc, "bnp", [128, 1024], F32))
        ld = [c.dma("sp", lambda e: e.dma_start(out=qb[:], in_=g.qbT[b].rearrange("h p t -> p h t")), "bld")]
        for gq in range(2):
            for half in range(2):
                ld.append(c.dma("sp", lambda e, gq=gq, half=half: e.dma_start(
                    out=kb[half * 64:(half + 1) * 64, gq, :], in_=g.kbT[b][gq * 64:(gq + 1) * 64, :]), "bld"))
            for dup in range(2):
                ld.append(c.dma("sp", lambda e, gq=gq, dup=dup: e.dma_start(
                    out=vb[:, :, gq, dup * 64:(dup + 1) * 64],
                    in_=g.vtok[b, :, 128 + gq * 64:128 + (gq + 1) * 64].rearrange("(n p) d -> p n d", p=128)), "bld"))
        ld.append(c.dma("sp", lambda e: e.dma_start(
            out=sk[:], in_=g.w["sinks"][l:l + 1, :].to_broadcast([128, 16])), "bld"))
        t_ld = ld[-1]
        t_sk = c.op("act", lambda e: e.activation(out=ske[:], in_=sk[:], func=AF.Exp), deps=[t_ld])
        lps_free = [None, None]
        ex_free = [None, None]
        pm_free = [None, None]
        ops_free = None
        nps_free = None
        yo_st = [None, None]
        nl = 0
        ny = 0
        for i in range(min(NBLK, SWA_DBG)):
            tq = slice(i * 128, (i + 1) * 128)
            chunks = ([(i - 1, g.mprev)] if i > 0 else []) + [(i, g.mcur)]
            for gq in range(2):
                t_pv = None
                for ci, (sc, msk) in enumerate(chunks):
                    li = nl % 2
                    nl += 1
                    ks = slice(sc * 128, (sc + 1) * 128)
                    for r in range(8):
                        h = gq * 8 + r
                        po = (h % 2) * 64
                        cp = (r % 2) * 4 + r // 2
                        t_l = c.op("pe", lambda e, li=li, po=po, gq=gq, ks=ks, h=h, cp=cp, tq=tq: e.matmul(
                            lps[li][:, cp * 128:(cp + 1) * 128], kb[po:po + 64, gq, ks], qb[po:po + 64, h // 2, tq],
                            start=True, stop=True), deps=[t_ld, lps_free[li]], sig=(r == 7))
                    t_e = c.op("act", lambda e, li=li: e.activation(
                        out=ex[li][:], in_=lps[li][:].rearrange("p (h t) -> p h t", h=8), func=AF.Exp),
                        deps=[t_l, ex_free[li]])
                    lps_free[li] = t_e
                    t_p = c.op("pool", lambda e, li=li, msk=msk: e.tensor_tensor(
                        out=pm[li][:], in0=ex[li][:], in1=bc(msk[:], [128, 8, 128], 1), op=ALU.mult),
                        deps=[t_e, pm_free[li]])
                    ex_free[li] = t_p
                    first, last = (ci == 0), (ci == len(chunks) - 1)
                    for hf in range(2):
                        c.op("pe", lambda e, li=li, sc=sc, gq=gq, hf=hf, first=first, last=last: e.matmul(
                            ops_[:, hf * 512:(hf + 1) * 512], vb[:, sc, gq, :], pm[li][:, hf * 4:(hf + 1) * 4, :],
                            start=first, stop=last), deps=[t_p, ops_free if first else None], sig=False)
                    for hf in range(2):
                        t_pv = c.op("pe", lambda e, li=li, hf=hf, first=first, last=last: e.matmul(
                            nps[:, hf * 512:(hf + 1) * 512], g.onesB[:], pm[li][:, hf * 4:(hf + 1) * 4, :],
                            start=first, stop=last), deps=[t_p, nps_free if first else None], sig=(hf == 1))
                    pm_free[li] = t_pv
                t_dn = c.op("dve", lambda e, gq=gq: e.tensor_tensor(
                    out=rc[:].rearrange("p (two r2) t -> p two r2 t", two=2),
                    in0=nps[:].rearrange("p (two r2 t) -> p two r2 t", two=2, r2=4),
                    in1=ske[:, gq * 8:(gq + 1) * 8].rearrange("p (r2 two) -> p two r2", two=2).unsqueeze(3)
                    .to_broadcast([128, 2, 4, 128]), op=ALU.add), deps=[t_pv, t_sk])
                nps_free = t_dn
                t_rc = c.op("dve", lambda e: e.reciprocal(rc[:], rc[:]), deps=[t_dn])
                yb = ny % 2
                ny += 1
                t_y1 = c.op("dve", lambda e, yb=yb: e.tensor_tensor(
                    out=yo[yb][:], in0=ops_[:].rearrange("p (r t) -> p r t", r=8), in1=rc[:], op=ALU.mult),
                    deps=[t_rc, yo_st[yb]])
                ops_free = t_y1
                y4 = yo[yb][:].rearrange("p (two r2) t -> p two r2 t", two=2)
                c.dma("sp", lambda e, y4=y4, gq=gq, tq=tq: e.dma_start(
                    out=g.ybT[b, gq * 4:(gq + 1) * 4, 0:64, tq].rearrange("h p t -> p h t"), in_=y4[0:64, 0, :, :]),
                    f"byo{yb}", deps=[t_y1], final=True)
                yo_st[yb] = c.dma("sp", lambda e, y4=y4, gq=gq, tq=tq: e.dma_start(
                    out=g.ybT[b, gq * 4:(gq + 1) * 4, 64:128, tq].rearrange("h p t -> p h t"),
                    in_=y4[64:128, 1, :, :]), f"byo{yb}", deps=[t_y1], final=True)
        c.run()


def blk_merge(nc, g, name, l, b, tok0, u, merged):
    c = BlockCtx(nc, name)
    win = g.w["w_in"][l]
    NP = TT // 512
    with ExitStack() as st:
        ya = st.enter_context(_sb(nc, "mya", [128, 8, TT], BF16))
        yb_ = st.enter_context(_sb(nc, "myb", [128, 8, TT], BF16))
        woa = [st.enter_context(_sb(nc, f"mwoa{i}", [128, 8, 256], BF16)) for i in range(2)]
        wob = [st.enter_context(_sb(nc, f"mwob{i}", [128, 8, 256], BF16)) for i in range(2)]
        wga = [st.enter_context(_sb(nc, f"mwga{i}", [128, KC, 256], BF16)) for i in range(2)]
        wgb = [st.enter_context(_sb(nc, f"mwgb{i}", [128, KC, 256], BF16)) for i in range(2)]
        sga = [st.enter_context(_sb(nc, f"msga{i}", [128, 512], F32)) for i in range(2)]
        sgb = [st.enter_context(_sb(nc, f"msgb{i}", [128, 512], F32)) for i in range(2)]
        aps_ = [st.enter_context(_ps(nc, f"mpa{i}", [128, 512], F32)) for i in range(2)]
        bps_ = [st.enter_context(_ps(nc, f"mpb{i}", [128, 512], F32)) for i in range(2)]
        gaps = [st.enter_context(_ps(nc, f"mpga{i}", [128, 512], F32)) for i in range(2)]
        gbps = [st.enter_context(_ps(nc, f"mpgb{i}", [128, 512], F32)) for i in range(2)]
        t_ld = c.dma("sp", lambda e: e.dma_start(
            out=ya[:], in_=g.yaT[b, :, :, tok0:tok0 + TT].rearrange("h p t -> p h t")), "mld")
        t_ld = c.dma("sp", lambda e: e.dma_start(
            out=yb_[:], in_=g.ybT[b, :, :, tok0:tok0 + TT].rearrange("h p t -> p h t")), "mld")
        w_rd = [None, None]
        fr = {"a": [None, None], "b": [None, None], "ga": [None, None], "gb": [None, None]}
        sg_free = [None, None]
        n = 0
        t_w = None
        for nch in range(KC):
            stg, col = nch // 2, (nch % 2) * 128
            bf = stg % 2
            if nch % 2 == 0:
                cs = slice(stg * 256, (stg + 1) * 256)
                for (dst, src, ck) in ((woa, g.w["w_o_a"][l][:, cs], "k"), (wob, g.w["w_o_b"][l][:, cs], "k"),
                                       (wga, win[:, O_GA + stg * 256:O_GA + (stg + 1) * 256], "k"),
                                       (wgb, win[:, O_GB + stg * 256:O_GB + (stg + 1) * 256], "k")):
                    t_w = c.dma("pool", lambda e, dst=dst, src=src, bf=bf: e.dma_start(
                        out=dst[bf][:], in_=src.rearrange("(k p) n -> p k n", p=128)), f"mw{bf}", deps=[w_rd[bf]])
            for pi in range(NP):
                i = n % 2
                n += 1
                tsl = slice(pi * 512, (pi + 1) * 512)
                for k in range(8):
                    t_a = c.op("pe", lambda e, i=i, bf=bf, k=k, col=col, tsl=tsl: e.matmul(
                        aps_[i][:], woa[bf][:, k, col:col + 128], ya[:, k, tsl], start=(k == 0), stop=(k == 7)),
                        deps=[t_w, t_ld, fr["a"][i]], sig=(k == 7))
                for k in range(8):
                    t_b = c.op("pe", lambda e, i=i, bf=bf, k=k, col=col, tsl=tsl: e.matmul(
                        bps_[i][:], wob[bf][:, k, col:col + 128], yb_[:, k, tsl], start=(k == 0), stop=(k == 7)),
                        deps=[t_w, t_ld, fr["b"][i]], sig=(k == 7))
                for k in range(KC):
                    t_ga = c.op("pe", lambda e, i=i, bf=bf, k=k, col=col, tsl=tsl: e.matmul(
                        gaps[i][:], wga[bf][:, k, col:col + 128], u[:, k, tsl], start=(k == 0), stop=(k == KC - 1)),
                        deps=[t_w, fr["ga"][i]], sig=(k == KC - 1))
                for k in range(KC):
                    t_gb = c.op("pe", lambda e, i=i, bf=bf, k=k, col=col, tsl=tsl: e.matmul(
                        gbps[i][:], wgb[bf][:, k, col:col + 128], u[:, k, tsl], start=(k == 0), stop=(k == KC - 1)),
                        deps=[t_w, fr["gb"][i]], sig=(k == KC - 1))
                t_last = t_gb
                t_s1 = c.op("act", lambda e, i=i: e.activation(out=sga[i][:], in_=gaps[i][:], func=AF.Sigmoid),
                            deps=[t_ga, sg_free[i]])
                t_s2 = c.op("act", lambda e, i=i: e.activation(out=sgb[i][:], in_=gbps[i][:], func=AF.Sigmoid),
                            deps=[t_gb, sg_free[i]])
                fr["ga"][i] = t_s1
                fr["gb"][i] = t_s2
                t_m1 = c.op("dve", lambda e, i=i: e.tensor_tensor(out=sga[i][:], in0=sga[i][:], in1=aps_[i][:],
                                                                  op=ALU.mult), deps=[t_s1, t_a])
                fr["a"][i] = t_m1
                t_m2 = c.op("dve", lambda e, i=i: e.tensor_tensor(out=sgb[i][:], in0=sgb[i][:], in1=bps_[i][:],
                                                                  op=ALU.mult), deps=[t_s2, t_b])
                fr["b"][i] = t_m2
                t_m3 = c.op("pool", lambda e, i=i, nch=nch, tsl=tsl: e.tensor_tensor(
                    out=merged[:, nch, tsl], in0=sga[i][:], in1=sgb[i][:], op=ALU.add), deps=[t_m1, t_m2])
                sg_free[i] = t_m3
            if nch % 2 == 1:
                w_rd[bf] = t_last
        c.run()


WEIGHT_NAMES = ["ada_w", "ada_b", "norm_ffn1_g", "ffn1_w_gate", "ffn1_w_up", "ffn1_w_down", "norm_mix_g",
                "w_in", "qn_a_g", "kn_a_g", "qn_b_g", "kn_b_g", "sinks", "w_o_a", "w_o_b", "w_out",
                "norm_ffn2_g", "ffn2_w_gate", "ffn2_w_up", "ffn2_w_down"]
WEIGHT_SHAPES = {
    "ada_w": [L, D, 9 * D], "ada_b": [L, 9 * D], "norm_ffn1_g": [L, D], "ffn1_w_gate": [L, D, FF],
    "ffn1_w_up": [L, D, FF], "ffn1_w_down": [L, FF, D], "norm_mix_g": [L, D], "w_in": [L, D, N_IN],
    "qn_a_g": [L, 128], "kn_a_g": [L, 128], "qn_b_g": [L, 64], "kn_b_g": [L, 64], "sinks": [L, 16],
    "w_o_a": [L, 1024, D], "w_o_b": [L, 1024, D], "w_out": [L, D, D], "norm_ffn2_g": [L, D],
    "ffn2_w_gate": [L, D, FF], "ffn2_w_up": [L, D, FF], "ffn2_w_down": [L, FF, D]}


FFN_PARTS = ("n", "g", "d")
DBG_TILES = 99
NWB = 2
DBG_NSEQ = NSEQ


def ffn_sublayer(nc, g, l, which, b, u, hidden):
    j = 0 if which == 1 else 2
    wg = g.w[f"ffn{which}_w_gate"][l]
    wu = g.w[f"ffn{which}_w_up"][l]
    wd = g.w[f"ffn{which}_w_down"][l]
    for tt in range(min(S // TT, DBG_TILES)):
        tok0 = tt * TT
        nm = f"l{l}f{which}b{b}t{tt}"
        if "n" in FFN_PARTS:
            blk_norm(nc, g, nm + "n", b, j, tok0, TT, u)
        if "g" in FFN_PARTS:
            blk_gu(nc, g, nm + "g", wg, wu, u, hidden, TT)
        if "d" in FFN_PARTS:
            blk_down(nc, g, nm + "d", wd, hidden, b, j, tok0, TT)


SWA_DBG = 99
MIX_PARTS = ("p", "a", "b", "m")
PROJ_DBG = ("qa", "ka", "qi", "ki", "qb", "kb", "tm")


def mix_sublayer(nc, g, l, b, u):
    nm = f"l{l}mb{b}"
    if "p" in MIX_PARTS:
        for hf in range(S // TT):
            blk_norm(nc, g, f"{nm}n{hf}", b, 1, hf * TT, TT, u)
            blk_proj(nc, g, f"{nm}p{hf}", l, b, hf * TT, u)
    if "a" in MIX_PARTS:
        blk_attn_a(nc, g, nm + "a", b)
    if "b" in MIX_PARTS:
        blk_swa(nc, g, nm + "s", l, b)
    if "m" in MIX_PARTS:
        with _sb(nc, "merged", [128, KC, TT], BF16) as merged:
            for hf in range(S // TT):
                blk_norm(nc, g, f"{nm}q{hf}", b, 1, hf * TT, TT, u)
                blk_merge(nc, g, f"{nm}m{hf}", l, b, hf * TT, u, merged)
                blk_down(nc, g, f"{nm}o{hf}", g.w["w_out"][l], merged, b, 1, hf * TT, TT, nfc=KC)


def build_program(n_layers=L, stages=("ffn1", "mix", "ffn2"), parts=("init", "pre", "post")):
    nc = bass.Bass("TRN2", target_bir_lowering=False)
    g = G()
    g.x_in = nc.dram_tensor("x", [NSEQ, S, D], F32, kind="ExternalInput").ap()
    g.c_in = nc.dram_tensor("c", [NSEQ, D], F32, kind="ExternalInput").ap()
    g.pos_in = nc.dram_tensor("positions", [NSEQ, S], I32, kind="ExternalInput").ap()
    g.consts_in = nc.dram_tensor("consts", [128, 8], F32, kind="ExternalInput").ap()
    g.w = {n: nc.dram_tensor(n, WEIGHT_SHAPES[n], F32, kind="ExternalInput").ap() for n in WEIGHT_NAMES}
    g.ada_w, g.ada_b = g.w["ada_w"], g.w["ada_b"]
    g.norm_g = [g.w["norm_ffn1_g"], g.w["norm_mix_g"], g.w["norm_ffn2_g"]]
    g.out = nc.dram_tensor("out", [NSEQ, S, D], F32, kind="ExternalOutput").ap()
    g.hT = nc.dram_tensor("hT", [NSEQ, D, S], F32).ap()
    g.rope = nc.dram_tensor("rope", [NSEQ, 6, 128, S], F32).ap()
    for nm_ in ("qaT", "qiT", "qbT", "yaT", "ybT"):
        setattr(g, nm_, nc.dram_tensor(nm_, [NSEQ, 8, 128, S], BF16).ap())
    for nm_ in ("kaT", "kiT", "kbT"):
        setattr(g, nm_, nc.dram_tensor(nm_, [NSEQ, 128, S], BF16).ap())
    g.vtok = nc.dram_tensor("vtok", [NSEQ, S, 256], BF16).ap()
    g.witok = nc.dram_tensor("witok", [NSEQ, S, 16], F32).ap()
    with ExitStack() as st:
        def sb(name, shape, dt):
            return st.enter_context(nc.sbuf_tensor(name, shape, dt))
        g.identF = sb("identF", [128, 128], F32)
        g.identB = sb("identB", [128, 128], BF16)
        g.onesF = sb("onesF", [128, 128], F32)
        g.epsc = sb("epsc", [128, 1], F32)
        g.cactT = sb("cactT", [128, KC, NSEQ], BF16)
        g.mod = sb("mod", [128, 9 * KC, NSEQ], F32)
        g.normg = sb("normg", [128, 3 * KC], F32)
        g.gsc = sb("gsc", [128, 3, KC, NSEQ], F32)
        g.gt = sb("gt", [128, 3, KC, NSEQ], F32)
        g.permA = sb("permA", [128, 128], F32)
        g.permB = sb("permB", [128, 128], F32)
        g.permI = sb("permI", [128, 128], F32)
        g.onesBD = sb("onesBD", [128, 128], F32)
        g.onesB = sb("onesB", [128, 128], BF16)
        g.cbias = sb("cbias", [128, 128], F32)
        g.mcurF = sb("mcurF", [128, 128], F32)
        g.mprevF = sb("mprevF", [128, 128], F32)
        g.mcur = sb("mcur", [128, 128], BF16)
        g.mprev = sb("mprev", [128, 128], BF16)
        g.cols = sb("cols", [128, 8], F32)
        u = sb("u", [128, KC, TT], BF16)
        if "init" in parts:
            blk_init(nc, g)
            blk_init2(nc, g)
        for b in range(NSEQ):
            if "pre" in parts:
                blk_pre(nc, g, b)
            if "mix" in stages and b < DBG_NSEQ:
                blk_rope(nc, g, b)
        for l in range(n_layers):
            blk_mod(nc, g, l)
            for b in range(DBG_NSEQ):
                if "ffn1" in stages:
                    with _sb(nc, "hidden", [128, FC, TT], BF16) as hidden:
                        ffn_sublayer(nc, g, l, 1, b, u, hidden)
                if "mix" in stages:
                    mix_sublayer(nc, g, l, b, u)
                if "ffn2" in stages:
                    with _sb(nc, "hidden", [128, FC, TT], BF16) as hidden:
                        ffn_sublayer(nc, g, l, 2, b, u, hidden)
        for b in range(NSEQ):
            if "post" in parts:
                blk_post(nc, g, b)
    return nc


def kernel(**inputs):
    n = 8
    nc = build_program()
    in_maps = []
    x = np.ascontiguousarray(inputs["x"], dtype=np.float32)
    c = np.ascontiguousarray(inputs["c"], dtype=np.float32)
    pos = np.ascontiguousarray(inputs["positions"], dtype=np.int32)
    ws = {k: np.ascontiguousarray(inputs[k], dtype=np.float32) for k in WEIGHT_NAMES}
    consts = host_consts()
    for i in range(n):
        m = {"x": x[NSEQ * i:NSEQ * (i + 1)], "c": c[NSEQ * i:NSEQ * (i + 1)],
             "positions": pos[NSEQ * i:NSEQ * (i + 1)], "consts": consts}
        m.update(ws)
        in_maps.append(m)
    res = run_bass_kernel_spmd(nc, in_maps, core_ids=list(range(n)))
    return np.concatenate([r["out"] for r in res.results], axis=0)
```
